# Optimizing a Trainium2 kernel written in Bass

```python
import jax, jax.numpy as jnp
from jax import lax
import numpy as np

D_MODEL = 1024
BATCH = 8
SEQ = 4096
DEPTH = 4

HEAD_DIM = 64
ROPE_THETA = 10000.0
NORM_EPS = 1e-6
NEG_INF = -1e30
D_FF = 2816

DIL_HEADS = 8
DIL_CONFIGS = ((128, 1), (512, 4), (2048, 16))
NA_HEADS = 8
GRID_W = 64
NA_ROWS = 8
NA_COLS = 16
NA_QCOLS = 16
NA_KCOLS = 32

SWA_Q_HEADS = 8
SWA_KV_HEADS = 2
SWA_HALF = 128
SWA_BLOCK = 128
MLA_HEADS = 8
MLA_Q_RANK = 384
MLA_KV_RANK = 256
MLA_NOPE = 64
MLA_ROPE = 32
MLA_V = 64
MLA_Q_BLOCK = 128

EVEN_SPLITS = (DIL_HEADS * HEAD_DIM,) * 3 + (NA_HEADS * HEAD_DIM,) * 3
EVEN_IN = sum(EVEN_SPLITS)
EVEN_MIX = (DIL_HEADS + NA_HEADS) * HEAD_DIM
ODD_SPLITS = (SWA_Q_HEADS * HEAD_DIM, SWA_KV_HEADS * HEAD_DIM, SWA_KV_HEADS * HEAD_DIM,
              MLA_Q_RANK, MLA_KV_RANK, MLA_ROPE)
ODD_IN = sum(ODD_SPLITS)
ODD_MIX = SWA_Q_HEADS * HEAD_DIM + MLA_HEADS * MLA_V

kernel_name = 'hybrid_bidir_encoder'


def rms_norm(x, g):
    xf = x.astype(jnp.float32)
    y = xf * lax.rsqrt(jnp.mean(xf * xf, axis=-1, keepdims=True) + NORM_EPS)
    return (y * g.astype(jnp.float32)).astype(x.dtype)


def rope_tables(seq, dim):
    pos = jnp.arange(seq, dtype=jnp.float32)
    inv_freq = ROPE_THETA ** (-jnp.arange(0, dim, 2, dtype=jnp.float32) / dim)
    ang = pos[:, None] * inv_freq[None, :]
    return jnp.cos(ang), jnp.sin(ang)


def apply_rope(x, cos, sin):
    half = x.shape[-1] // 2
    bshape = (cos.shape[0],) + (1,) * (x.ndim - 3) + (half,)
    c, s = cos.reshape(bshape), sin.reshape(bshape)
    xf = x.astype(jnp.float32)
    x1, x2 = xf[..., :half], xf[..., half:]
    return jnp.concatenate([x1 * c - x2 * s, x2 * c + x1 * s], axis=-1).astype(x.dtype)


def swiglu(x, w1, w3, w2):
    return (jax.nn.silu(x @ w1) * (x @ w3)) @ w2


def banded_window_attn(q, k, v, half, block, sink=None):
    n, L, hk, g, dh = q.shape
    nb = -(-L // block)
    lp = nb * block
    scale = dh ** -0.5
    qb = jnp.pad(q, ((0, 0), (0, lp - L), (0, 0), (0, 0), (0, 0))).reshape(n, nb, block, hk, g, dh)
    pad_kv = ((0, 0), (block, lp - L + block), (0, 0), (0, 0))
    kp = jnp.pad(k, pad_kv).reshape(n, nb + 2, block, hk, dh)
    vp = jnp.pad(v, pad_kv).reshape(n, nb + 2, block, hk, dh)
    kb = jnp.concatenate([kp[:, :-2], kp[:, 1:-1], kp[:, 2:]], axis=2)
    vb = jnp.concatenate([vp[:, :-2], vp[:, 1:-1], vp[:, 2:]], axis=2)
    sc = jnp.einsum('nbqhgd,nbkhd->nbhgqk', qb, kb, preferred_element_type=jnp.float32) * scale
    qpos = np.arange(nb)[:, None] * block + np.arange(block)[None, :]
    kpos = (np.arange(nb)[:, None] - 1) * block + np.arange(3 * block)[None, :]
    valid = ((np.abs(qpos[:, :, None] - kpos[:, None, :]) <= half)
             & (kpos[:, None, :] >= 0) & (kpos[:, None, :] < L))
    sc = jnp.where(valid[None, :, None, None], sc, NEG_INF)
    m = jnp.max(sc, axis=-1)
    if sink is not None:
        sk = sink.astype(jnp.float32).reshape(1, 1, hk, g, 1)
        m = jnp.maximum(m, sk)
    p = jnp.exp(sc - m[..., None])
    denom = jnp.sum(p, axis=-1)
    if sink is not None:
        denom = denom + jnp.exp(sk - m)
    o = jnp.einsum('nbhgqk,nbkhd->nbqhgd', p, vb.astype(jnp.float32))
    o = o / jnp.moveaxis(denom, -1, 2)[..., None]
    lse = jnp.moveaxis(m + jnp.log(denom), -1, 2)
    o = o.reshape(n, lp, hk, g, dh)[:, :L].astype(q.dtype)
    lse = lse.reshape(n, lp, hk, g)[:, :L]
    return o, lse


def dilated_attention(q, k, v):
    b, s, h, dh = q.shape
    outs, lses = [], []
    for window, dil in DIL_CONFIGS:
        half = window // 2 // dil
        sd = s // dil

        def to_res(t):
            return t.reshape(b, sd, dil, h, dh).transpose(0, 2, 1, 3, 4).reshape(b * dil, sd, h, dh)

        o, lse = banded_window_attn(to_res(q)[:, :, :, None], to_res(k), to_res(v), half, half)
        outs.append(o[:, :, :, 0].reshape(b, dil, sd, h, dh).transpose(0, 2, 1, 3, 4).reshape(b, s, h, dh))
        lses.append(lse[..., 0].reshape(b, dil, sd, h).transpose(0, 2, 1, 3).reshape(b, s, h))
    w = jax.nn.softmax(jnp.stack(lses), axis=0)
    out = jnp.einsum('cbsh,cbshd->bshd', w, jnp.stack(outs).astype(jnp.float32))
    return out.astype(q.dtype)


def neighbourhood_attention(q, k, v, rpb):
    b, s, h, dh = q.shape
    rows = s // GRID_W
    kr = min(NA_ROWS, rows)
    ncb = GRID_W // NA_QCOLS
    scale = dh ** -0.5
    qg = q.reshape(b, rows, GRID_W, h, dh)
    kg = k.reshape(b, rows, GRID_W, h, dh)
    vg = v.reshape(b, rows, GRID_W, h, dh)
    qcol = np.arange(GRID_W).reshape(ncb, NA_QCOLS)
    kstart = np.clip(np.arange(ncb) * NA_QCOLS - NA_COLS // 2, 0, GRID_W - NA_KCOLS)
    kcol = kstart[:, None] + np.arange(NA_KCOLS)[None, :]
    wstart = np.clip(qcol - NA_COLS // 2, 0, GRID_W - NA_COLS)
    col_valid = ((kcol[:, None, :] >= wstart[..., None])
                 & (kcol[:, None, :] < wstart[..., None] + NA_COLS))
    dc_idx = np.clip(kcol[:, None, :] - qcol[..., None] + NA_COLS - 1, 0, 2 * NA_COLS - 2)
    rpb_c = rpb[:, :, dc_idx]

    def row_step(r):
        rs = jnp.clip(r - kr // 2, 0, rows - kr)
        k_rows = lax.dynamic_slice_in_dim(kg, rs, kr, axis=1)[:, :, kcol]
        v_rows = lax.dynamic_slice_in_dim(vg, rs, kr, axis=1)[:, :, kcol]
        q_row = lax.dynamic_index_in_dim(qg, r, axis=1, keepdims=False).reshape(b, ncb, NA_QCOLS, h, dh)
        sc = jnp.einsum('bmqhd,brmkhd->bhmqrk', q_row, k_rows, preferred_element_type=jnp.float32) * scale
        dr_idx = rs - r + jnp.arange(kr) + NA_ROWS - 1
        bias = jnp.take(rpb_c, dr_idx, axis=1).transpose(0, 2, 3, 1, 4)
        sc = sc + bias[None].astype(jnp.float32)
        sc = jnp.where(col_valid[None, None, :, :, None, :], sc, NEG_INF)
        p = jax.nn.softmax(sc, axis=(-2, -1))
        o = jnp.einsum('bhmqrk,brmkhd->bmqhd', p, v_rows.astype(jnp.float32))
        return o.reshape(b, GRID_W, h, dh).astype(q.dtype)

    out = lax.map(row_step, jnp.arange(rows))
    return out.transpose(1, 0, 2, 3, 4).reshape(b, s, h, dh)


def mla_attention(q_a, kv_a, k_pe, q_norm, w_qb, kv_norm, w_kvb, cos_r, sin_r):
    b, s, _ = q_a.shape
    q = (rms_norm(q_a, q_norm) @ w_qb).reshape(b, s, MLA_HEADS, MLA_NOPE + MLA_ROPE)
    q_nope = q[..., :MLA_NOPE]
    q_pe = apply_rope(q[..., MLA_NOPE:], cos_r, sin_r)
    kv = (rms_norm(kv_a, kv_norm) @ w_kvb).reshape(b, s, MLA_HEADS, MLA_NOPE + MLA_V)
    k_nope = kv[..., :MLA_NOPE]
    v_f = kv[..., MLA_NOPE:].astype(jnp.float32)
    k_pe = apply_rope(k_pe, cos_r, sin_r)
    scale = (MLA_NOPE + MLA_ROPE) ** -0.5
    nqb = s // MLA_Q_BLOCK

    def blocks(t):
        return t.reshape((b, nqb, MLA_Q_BLOCK) + t.shape[2:]).swapaxes(0, 1)

    def step(args):
        qn, qp = args
        sc = (jnp.einsum('bqhd,bkhd->bhqk', qn, k_nope, preferred_element_type=jnp.float32)
              + jnp.einsum('bqhr,bkr->bhqk', qp, k_pe, preferred_element_type=jnp.float32)) * scale
        p = jax.nn.softmax(sc, axis=-1)
        return jnp.einsum('bhqk,bkhd->bqhd', p, v_f)

    o = lax.map(step, (blocks(q_nope), blocks(q_pe)))
    return o.swapaxes(0, 1).reshape(b, s, MLA_HEADS * MLA_V).astype(q_a.dtype)


def even_mixer(h, w_in, w_out, rpb, cos, sin):
    b, s, _ = h.shape
    proj = h @ w_in
    qa, ka, va, qn, kn, vn = jnp.split(proj, np.cumsum(EVEN_SPLITS[:-1]).tolist(), axis=-1)
    qa = apply_rope(qa.reshape(b, s, DIL_HEADS, HEAD_DIM), cos, sin)
    ka = apply_rope(ka.reshape(b, s, DIL_HEADS, HEAD_DIM), cos, sin)
    oa = dilated_attention(qa, ka, va.reshape(b, s, DIL_HEADS, HEAD_DIM))
    on = neighbourhood_attention(qn.reshape(b, s, NA_HEADS, HEAD_DIM), kn.reshape(b, s, NA_HEADS, HEAD_DIM),
                                 vn.reshape(b, s, NA_HEADS, HEAD_DIM), rpb)
    mixed = jnp.concatenate([oa.reshape(b, s, DIL_HEADS * HEAD_DIM), on.reshape(b, s, NA_HEADS * HEAD_DIM)], axis=-1)
    return mixed @ w_out


def odd_mixer(h, w_in, w_out, sink, q_norm, w_qb, kv_norm, w_kvb, cos, sin, cos_r, sin_r):
    b, s, _ = h.shape
    proj = h @ w_in
    q_c, k_c, v_c, q_a, kv_a, k_pe = jnp.split(proj, np.cumsum(ODD_SPLITS[:-1]).tolist(), axis=-1)
    grp = SWA_Q_HEADS // SWA_KV_HEADS
    qc = apply_rope(q_c.reshape(b, s, SWA_Q_HEADS, HEAD_DIM), cos, sin).reshape(b, s, SWA_KV_HEADS, grp, HEAD_DIM)
    kc = apply_rope(k_c.reshape(b, s, SWA_KV_HEADS, HEAD_DIM), cos, sin)
    vc = v_c.reshape(b, s, SWA_KV_HEADS, HEAD_DIM)
    oc, _ = banded_window_attn(qc, kc, vc, SWA_HALF, SWA_BLOCK, sink.reshape(SWA_KV_HEADS, grp))
    od = mla_attention(q_a, kv_a, k_pe, q_norm, w_qb, kv_norm, w_kvb, cos_r, sin_r)
    mixed = jnp.concatenate([oc.reshape(b, s, SWA_Q_HEADS * HEAD_DIM), od], axis=-1)
    return mixed @ w_out


def setup_inputs(seed: int = 0) -> dict:
    key = jax.random.key(seed)
    ks = jax.random.split(key, 21)
    n_even, n_odd = (DEPTH + 1) // 2, DEPTH // 2

    def dense(k, shape, fan_in):
        return jax.random.normal(k, shape, jnp.float32) * fan_in ** -0.5

    def gain(k, shape):
        return 1.0 + 0.02 * jax.random.normal(k, shape, jnp.float32)

    return {
        'x': jax.random.normal(ks[0], (BATCH, SEQ, D_MODEL), jnp.float32),
        'ffn1_norm': gain(ks[1], (DEPTH, D_MODEL)),
        'ffn1_w1': dense(ks[2], (DEPTH, D_MODEL, D_FF), D_MODEL),
        'ffn1_w3': dense(ks[3], (DEPTH, D_MODEL, D_FF), D_MODEL),
        'ffn1_w2': dense(ks[4], (DEPTH, D_FF, D_MODEL), D_FF),
        'mix_norm': gain(ks[5], (DEPTH, D_MODEL)),
        'ffn2_norm': gain(ks[6], (DEPTH, D_MODEL)),
        'ffn2_w1': dense(ks[7], (DEPTH, D_MODEL, D_FF), D_MODEL),
        'ffn2_w3': dense(ks[8], (DEPTH, D_MODEL, D_FF), D_MODEL),
        'ffn2_w2': dense(ks[9], (DEPTH, D_FF, D_MODEL), D_FF),
        'even_w_in': dense(ks[10], (n_even, D_MODEL, EVEN_IN), D_MODEL),
        'even_w_out': dense(ks[11], (n_even, EVEN_MIX, D_MODEL), EVEN_MIX),
        'na_rel_bias': 0.1 * jax.random.normal(ks[12], (n_even, NA_HEADS, 2 * NA_ROWS - 1, 2 * NA_COLS - 1), jnp.float32),
        'odd_w_in': dense(ks[13], (n_odd, D_MODEL, ODD_IN), D_MODEL),
        'odd_w_out': dense(ks[14], (n_odd, ODD_MIX, D_MODEL), ODD_MIX),
        'swa_sink': jax.random.normal(ks[15], (n_odd, SWA_Q_HEADS), jnp.float32),
        'mla_q_norm': gain(ks[16], (n_odd, MLA_Q_RANK)),
        'mla_w_qb': dense(ks[17], (n_odd, MLA_Q_RANK, MLA_HEADS * (MLA_NOPE + MLA_ROPE)), MLA_Q_RANK),
        'mla_kv_norm': gain(ks[18], (n_odd, MLA_KV_RANK)),
        'mla_w_kvb': dense(ks[19], (n_odd, MLA_KV_RANK, MLA_HEADS * (MLA_NOPE + MLA_V)), MLA_KV_RANK),
        'final_norm': gain(ks[20], (D_MODEL,)),
    }


def reference(x, ffn1_norm, ffn1_w1, ffn1_w3, ffn1_w2, mix_norm, ffn2_norm, ffn2_w1, ffn2_w3, ffn2_w2,
              even_w_in, even_w_out, na_rel_bias, odd_w_in, odd_w_out, swa_sink,
              mla_q_norm, mla_w_qb, mla_kv_norm, mla_w_kvb, final_norm):
    s = x.shape[1]
    cos, sin = rope_tables(s, HEAD_DIM)
    cos_r, sin_r = rope_tables(s, MLA_ROPE)
    h = x
    for i in range(DEPTH):
        j = i // 2
        h = h + 0.5 * swiglu(rms_norm(h, ffn1_norm[i]), ffn1_w1[i], ffn1_w3[i], ffn1_w2[i])
        hn = rms_norm(h, mix_norm[i])
        if i % 2 == 0:
            h = h + even_mixer(hn, even_w_in[j], even_w_out[j], na_rel_bias[j], cos, sin)
        else:
            h = h + odd_mixer(hn, odd_w_in[j], odd_w_out[j], swa_sink[j], mla_q_norm[j], mla_w_qb[j],
                              mla_kv_norm[j], mla_w_kvb[j], cos, sin, cos_r, sin_r)
        h = h + 0.5 * swiglu(rms_norm(h, ffn2_norm[i]), ffn2_w1[i], ffn2_w3[i], ffn2_w2[i])
    return rms_norm(h, final_norm)
```

```python
import numpy as np
import ml_dtypes
from contextlib import ExitStack
import concourse.bass as bass
import concourse.mybir as mybir
from concourse.bass_utils import run_bass_kernel_spmd

F32 = mybir.dt.float32
BF16 = mybir.dt.bfloat16
AF = mybir.ActivationFunctionType
ALU = mybir.AluOpType

import os
DBG = int(os.environ.get("KDBG", "0"))
MASK_ENG = os.environ.get("KMASKENG", "dve")
SKIP_SAME = int(os.environ.get("KSKIPSAME", "0"))
SKIPFFN = int(os.environ.get("KSKIPFFN", "0"))
DBG2 = int(os.environ.get("KDBG2", "0"))
S = 4096
D = 1024
DFF = 2816
NT = S // 128
ARENA_KIB = 204


class Op:
    __slots__ = ("eng", "fn", "dma", "idx", "deps", "signal", "sem", "val", "ringdep", "used")


class Prog:
    ENGS = ("pe", "act", "dve", "pool", "sp")
    NRING = 8

    def __init__(self, nc):
        self.nc = nc
        self.ops = []
        self.lastw = {}
        self.readers = {}
        self.eng_ops = {e: [] for e in self.ENGS}
        self.ndma = {e: 0 for e in self.ENGS}
        self.dma_ops = {e: [] for e in self.ENGS}
        self.pending_bar = {e: [] for e in self.ENGS}

    def add(self, eng, fn, reads=(), writes=(), dma=False):
        op = Op()
        op.eng, op.fn, op.dma, op.idx = eng, fn, dma, len(self.ops)
        op.used = False
        px = [r for r in reads if isinstance(r, tuple) and r[0] == "ps"]
        if px:
            reads = [r for r in reads if not (isinstance(r, tuple) and r[0] == "ps")]
            writes = list(writes) + [r for r in px if r not in writes]
        deps = {}
        for r in reads:
            w = self.lastw.get(r)
            if w is not None:
                deps[w] = "raw"
        for r in writes:
            w = self.lastw.get(r)
            if w is not None:
                deps.setdefault(w, "waw")
            for rd in self.readers.get(r, ()):
                deps.setdefault(rd, "war")
        op.deps = []
        for d, kind in deps.items():
            o = self.ops[d]
            if SKIP_SAME and o.eng == eng and (not o.dma) and (not dma) and kind != "raw":
                continue
            op.deps.append(d)
            o.used = True
        if self.pending_bar[eng]:
            for d in self.pending_bar[eng]:
                if d not in op.deps:
                    op.deps.append(d)
            self.pending_bar[eng] = []
        op.ringdep = None
        if dma:
            k = self.ndma[eng]
            self.ndma[eng] = k + 1
            if k >= self.NRING:
                op.ringdep = self.dma_ops[eng][k - self.NRING]
            self.dma_ops[eng].append(op)
        for r in reads:
            self.readers.setdefault(r, []).append(op.idx)
        for r in writes:
            self.lastw[r] = op.idx
            self.readers[r] = []
        self.ops.append(op)
        self.eng_ops[eng].append(op)
        return op

    def barrier(self):
        last = [self.eng_ops[e][-1] for e in self.ENGS if self.eng_ops[e]]
        pend = []
        for e in self.ENGS:
            pend += self.dma_ops[e][-self.NRING:]
        deps = []
        for o in last + pend:
            o.used = True
            if o.idx not in deps:
                deps.append(o.idx)
        self.pending_bar = {e: list(deps) for e in self.ENGS}
        self.lastw = {}
        self.readers = {}

    def emit(self, stack):
        nc = self.nc
        esem = {e: stack.enter_context(nc.semaphore("s_" + e)) for e in self.ENGS}
        rings = {e: [stack.enter_context(nc.semaphore("r_%s%d" % (e, i))) for i in range(self.NRING)]
                 for e in self.ENGS if self.ndma[e]}
        cnt = {e: 0 for e in self.ENGS}
        dcnt = {e: 0 for e in self.ENGS}
        for op in self.ops:
            if op.dma:
                k = dcnt[op.eng]
                dcnt[op.eng] = k + 1
                op.sem = rings[op.eng][k % self.NRING]
                op.val = 16 * (k // self.NRING + 1)
                op.signal = True
            else:
                op.signal = op.used
                if op.signal:
                    cnt[op.eng] += 1
                op.sem = esem[op.eng]
                op.val = cnt[op.eng]
        ops = self.ops
        block = stack.enter_context(nc.Block())
        deco = {"pe": block.tensor, "act": block.scalar, "dve": block.vector,
                "pool": block.gpsimd, "sp": block.sync}

        def run(eng):
            def body(e):
                waited = {}
                for op in self.eng_ops[eng]:
                    needs = {}
                    lst = [ops[d] for d in op.deps]
                    if op.ringdep is not None:
                        lst.append(op.ringdep)
                    for o in lst:
                        if needs.get(id(o.sem), (None, 0))[1] < o.val:
                            needs[id(o.sem)] = (o.sem, o.val)
                    for sid, (sem, val) in needs.items():
                        if waited.get(sid, 0) >= val:
                            continue
                        e.wait_ge(sem, val)
                        waited[sid] = val
                    ins = op.fn(e)
                    if op.signal:
                        ins.then_inc(op.sem, 16 if op.dma else 1)
                if eng == "sp":
                    for en in self.ENGS:
                        for o in self.dma_ops[en][-self.NRING:]:
                            if waited.get(id(o.sem), 0) < o.val:
                                e.wait_ge(o.sem, o.val)
                                waited[id(o.sem)] = o.val
            deco[eng](body)

        for eng in self.ENGS:
            run(eng)


class Arena:
    def __init__(self, t16):
        self.t16 = t16
        self.t32 = t16.bitcast(F32)
        self.off = 0
        self.cap = ARENA_KIB * 1024

    def alloc(self, n, dt):
        sz = 2 if dt == BF16 else 4
        self.off = (self.off + 63) // 64 * 64
        o = self.off
        self.off += n * sz
        assert self.off <= self.cap, ("arena overflow", self.off)
        if dt == BF16:
            return self.t16[:, o // 2:o // 2 + n]
        return self.t32[:, o // 4:o // 4 + n]


def bc_last(ap, n):
    return bass.AP(ap.tensor, ap.offset, [list(p) for p in ap.ap] + [[0, n]])


def bc_mid(ap, n):
    a = [list(p) for p in ap.ap]
    return bass.AP(ap.tensor, ap.offset, [a[0], [0, n]] + a[1:])


def strided(ap2, start, step, cnt):
    a = [list(p) for p in ap2.ap]
    assert len(a) == 2 and a[1][0] == 1
    return bass.AP(ap2.tensor, ap2.offset + start, [a[0], [step, cnt]])


class Builder:
    def __init__(self, nsteps=12, final=True):
        self.nsteps = nsteps
        self.final = final

    def dram_in(self, name, shape, dt=F32):
        return self.nc.dram_tensor(name, list(shape), dt, kind="ExternalInput").ap()

    def build(self):
        nc = self.nc = bass.Bass("TRN2", target_bir_lowering=False)
        di = self.dram_in
        self.x = di("x", [S, D])
        self.gains = di("gains", [128, 96])
        self.fnorm = di("fnorm", [1, D])
        self.ffn_w1 = [di("ffn1_w1", [4, D, DFF]), di("ffn2_w1", [4, D, DFF])]
        self.ffn_w3 = [di("ffn1_w3", [4, D, DFF]), di("ffn2_w3", [4, D, DFF])]
        self.ffn_w2 = [di("ffn1_w2", [4, DFF, D]), di("ffn2_w2", [4, DFF, D])]
        self.even_w_in = di("even_w_in", [2, D, 3072])
        self.even_w_out = di("even_w_out", [2, D, D])
        self.odd_w_in = di("odd_w_in", [2, D, 1440])
        self.odd_w_out = di("odd_w_out", [2, D, D])
        self.w_qb = di("mla_w_qb", [2, 384, 768])
        self.w_kvb = di("mla_w_kvb", [2, 256, 1024])
        self.mla_g = di("mla_g", [128, 10])
        self.sink = di("sink", [1, 16])
        self.narpb = di("narpb", [2, 8, 2, 128, 896])
        self.namask = di("namask", [2, 128, 896])
        self.cst = di("cst", [6, 128, 128])
        self.rope64 = di("rope64", [2, 128, S])
        self.rope32 = di("rope32", [2, 128, S])
        self.out = nc.dram_tensor("out", [S, D], F32, kind="ExternalOutput").ap()
        self.vd = nc.dram_tensor("vd", [S, 8, 128], BF16, kind="ExternalOutput").ap()

        st = ExitStack()
        with st:
            t16 = st.enter_context(nc.sbuf_tensor("arena", [128, ARENA_KIB * 512], BF16))
            self.ar = Arena(t16)
            self.ps = [st.enter_context(nc.psum_tensor("ps%d" % i, [128, 512], F32)) for i in range(8)]
            self.P = Prog(nc)
            self.consts()
            step = 0
            src = self.x
            for i in range(4):
                for sub in range(3):
                    if step >= self.nsteps:
                        break
                    if sub == 0:
                        if not SKIPFFN:
                            self.ffn(i, 0, src)
                    elif sub == 1:
                        (self.even_mixer if i % 2 == 0 else self.odd_mixer)(i)
                    else:
                        self.ffn(i, 1, src)
                    src = self.out
                    step += 1
            if self.final:
                self.final_norm(src)
            self.P.emit(st)
        return nc

    def A(self, eng, fn, reads=(), writes=(), dma=False):
        return self.P.add(eng, fn, reads, writes, dma)

    def dma(self, out, in_, reads=(), writes=(), eng="sp"):
        return self.P.add(eng, lambda e: e.dma_start(out=out, in_=in_), reads, writes, dma=True)

    def mm(self, out, pairs, reads, writes):
        def fn(e):
            n = len(pairs)
            for i, (l, r) in enumerate(pairs):
                ins = e.matmul(out, lhsT=l, rhs=r, start=(i == 0), stop=(i == n - 1))
            return ins
        return self.P.add("pe", fn, reads, writes)

    def consts(self):
        ar = self.ar
        self.gt = ar.alloc(96, F32)
        self.mg = ar.alloc(16, F32)
        self.epsb = ar.alloc(8, F32)
        self.esink = ar.alloc(16, F32)
        self.swapf = ar.alloc(128, F32)
        self.ident = ar.alloc(128, BF16)
        self.R64 = ar.alloc(128, BF16)
        self.R32 = ar.alloc(128, BF16)
        self.M2 = ar.alloc(256, BF16)
        self.Mge = self.M2[:, 0:128]
        self.Mle = self.M2[:, 128:256]
        self.ones_col = ar.alloc(64, BF16)
        self.stat = ar.alloc(64, F32)
        self.dma(self.gt, self.gains, writes=["c_gt"])
        self.dma(self.mg[:, 0:10], self.mla_g, writes=["c_mg"])
        self.dma(self.swapf, self.cst[1], writes=["c_swap"])
        sk = bass.AP(self.sink.tensor, self.sink.offset, [[0, 128], [1, 16]])
        self.dma(self.esink, sk, writes=["c_sink"])
        for k, t in ((0, self.ident), (2, self.R64), (3, self.R32), (4, self.Mge), (5, self.Mle)):
            self.dma(t, self.cst[k], writes=["c_bf%d" % k], eng="pool")
        self.A("pool", lambda e: e.memset(self.epsb, 1e-6), writes=["c_eps"])
        self.A("pool", lambda e: e.memset(self.ones_col, 1.0), writes=["c_ones"])
        self.A("act", lambda e: e.activation(out=self.esink, in_=self.esink, func=AF.Exp),
               reads=["c_sink"], writes=["c_sink"])
        self.cmark = ar.off
        self.P.barrier()
        self.statk = 0

    def phase(self):
        self.P.barrier()
        self.ar.off = self.cmark

    def norm_tile(self, xt, xs, junk, rx, rxs, width=D, eps_scale=None):
        k = self.statk % 16
        self.statk += 1
        ss = self.stat[:, k:k + 1]
        rs = self.stat[:, 16 + k:17 + k]
        rr = self.stat[:, 32 + k:33 + k]
        inv = 1.0 / width
        self.A("dve", lambda e: e.scalar_tensor_tensor(out=junk, in0=xt, scalar=inv, in1=xt, op0=ALU.mult,
                                                         op1=ALU.mult, accum_out=ss),
               reads=[rx], writes=[("junk",), ("ss", k)])
        self.A("act", lambda e: e.activation(out=rs, in_=ss, func=AF.Sqrt, bias=self.epsb[:, 0:1], scale=1.0),
               reads=[("ss", k)], writes=[("rs", k)])
        self.A("dve", lambda e: e.reciprocal(out=rr, in_=rs), reads=[("rs", k)], writes=[("rr", k)])
        self.A("dve", lambda e: e.tensor_scalar(out=xs, in0=xt, scalar1=rr, scalar2=None, op0=ALU.mult),
               reads=[rx, ("rr", k)], writes=[rxs])
        return rr, ("rr", k)

    def alloc_norm_bufs(self, nxt=3, nh=1):
        ar = self.ar
        self.xt = [ar.alloc(D, F32) for _ in range(nxt)]
        self.xs = [ar.alloc(D, BF16) for _ in range(2)]
        self.junk = ar.alloc(D, BF16)
        self.hnTs = [ar.alloc(8 * 512, BF16).rearrange("p (c n) -> p c n", c=8) for _ in range(nh)]
        self.hnT = self.hnTs[0]
        self.ntile = 0

    def norm_chunk(self, src, c, g0, buf=0, tiles=(0, 1, 2, 3)):
        psT = self.ps[0].bitcast(BF16)
        for t in tiles:
            tile = 4 * c + t
            n = self.ntile
            self.ntile += 1
            xt = self.xt[n % len(self.xt)]
            xs = self.xs[n % 2]
            rx = ("xt", n % len(self.xt))
            rxs = ("xs", n % 2)
            self.dma(xt, src[tile * 128:(tile + 1) * 128, :], reads=[("H", tile)], writes=[rx])
            self.norm_tile(xt, xs, self.junk, rx, rxs)

            def tr(e, xs=xs):
                for cc in range(8):
                    ins = e.transpose(out=psT[:, cc * 128:(cc + 1) * 128], in_=xs[:, cc * 128:(cc + 1) * 128],
                                      identity=self.ident)
                return ins
            self.A("pe", tr, reads=[rxs], writes=[("ps", 0)])
            gb = bc_last(self.gt[:, g0:g0 + 8], 128)
            hdst = self.hnTs[buf][:, :, t * 128:(t + 1) * 128]
            self.A("dve", lambda e, hdst=hdst, gb=gb: e.tensor_tensor(
                out=hdst, in0=psT.rearrange("p (c n) -> p c n", c=8), in1=gb,
                op=ALU.mult), reads=[("ps", 0)], writes=[("hnT", buf)])

    def ffn(self, i, which, src):
        self.phase()
        ar = self.ar
        w1b = ar.alloc(8 * DFF, BF16).rearrange("p (c n) -> p c n", c=8)
        w3b = ar.alloc(8 * DFF, BF16).rearrange("p (c n) -> p c n", c=8)
        w2b = ar.alloc(22 * D, BF16).rearrange("p (c n) -> p c n", c=22)
        w1 = self.ffn_w1[which][i].rearrange("(c p) n -> p c n", p=128)
        w3 = self.ffn_w3[which][i].rearrange("(c p) n -> p c n", p=128)
        w2 = self.ffn_w2[which][i].rearrange("(c p) n -> p c n", p=128)
        for h in range(2):
            sl = slice(h * 1408, (h + 1) * 1408)
            self.dma(w1b[:, :, sl], w1[:, :, sl], writes=[("w1", h)], eng="pool")
            self.dma(w3b[:, :, sl], w3[:, :, sl], writes=[("w3", h)], eng="pool")
        for h in range(2):
            self.dma(w2b[:, h * 11:(h + 1) * 11, :], w2[:, h * 11:(h + 1) * 11, :], writes=[("w2", h)], eng="pool")
        self.alloc_norm_bufs(3)
        ht = [ar.alloc(D, F32) for _ in range(2)]
        gT = ar.alloc(22 * 512, BF16).rearrange("p (c n) -> p c n", c=22)
        slb = [ar.alloc(512, BF16) for _ in range(2)]
        g0 = (0 if which == 0 else 64) + 8 * i
        nres = 0
        self.norm_chunk(src, 0, g0)
        for c in range(8):
            for fc in range(22):
                b = fc % 2
                h = 0 if fc < 11 else 1
                p1, p3 = self.ps[1 + b], self.ps[3 + b]
                fs = slice(fc * 128, (fc + 1) * 128)
                self.mm(p1[:, :], [(w1b[:, dc, fs], self.hnT[:, dc, :]) for dc in range(8)],
                        reads=[("w1", h), ("hnT", 0)], writes=[("ps", 1 + b)])
                self.mm(p3[:, :], [(w3b[:, dc, fs], self.hnT[:, dc, :]) for dc in range(8)],
                        reads=[("w3", h), ("hnT", 0)], writes=[("ps", 3 + b)])
                self.A("act", lambda e, p1=p1, b=b: e.activation(out=slb[b], in_=p1[:, :], func=AF.Silu),
                       reads=[("ps", 1 + b)], writes=[("slb", b)])
                self.A("dve", lambda e, p3=p3, b=b, fc=fc: e.tensor_tensor(out=gT[:, fc, :], in0=slb[b], in1=p3[:, :],
                                                                           op=ALU.mult),
                       reads=[("ps", 3 + b), ("slb", b)], writes=[("gT", fc)])
            if c + 1 < 8:
                self.norm_chunk(src, c + 1, g0)
            for t in range(4):
                tile = 4 * c + t
                hb = nres % 2
                self.dma(ht[hb], src[tile * 128:(tile + 1) * 128, :], reads=[("H", tile)], writes=[("ht", hb)])
                for half in range(2):
                    pb = 5 + (2 * nres + half) % 3
                    po = self.ps[pb]
                    hs = slice(half * 512, (half + 1) * 512)
                    self.mm(po[:, :], [(gT[:, fc, t * 128:(t + 1) * 128], w2b[:, fc, hs]) for fc in range(22)],
                            reads=[("gT", fc) for fc in range(22)] + [("w2", 0), ("w2", 1)], writes=[("ps", pb)])
                    self.A("dve", lambda e, po=po, hb=hb, hs=hs: e.scalar_tensor_tensor(
                        out=ht[hb][:, hs], in0=po[:, :], scalar=0.5, in1=ht[hb][:, hs], op0=ALU.mult, op1=ALU.add),
                        reads=[("ps", pb), ("ht", hb)], writes=[("ht", hb)])
                self.dma(self.out[tile * 128:(tile + 1) * 128, :], ht[hb], reads=[("ht", hb)], writes=[("H", tile)])
                nres += 1

    def final_norm(self, src):
        self.phase()
        ar = self.ar
        gf = ar.alloc(D, F32)
        fb = bass.AP(self.fnorm.tensor, self.fnorm.offset, [[0, 128], [1, D]])
        self.dma(gf, fb, writes=["gf"])
        xt = [ar.alloc(D, F32) for _ in range(3)]
        junk = ar.alloc(D, BF16)
        for tile in range(NT):
            b = tile % 3
            k = self.statk % 16
            self.statk += 1
            ss = self.stat[:, k:k + 1]
            rs = self.stat[:, 16 + k:17 + k]
            rr = self.stat[:, 32 + k:33 + k]
            self.dma(xt[b], src[tile * 128:(tile + 1) * 128, :], reads=[("H", tile)], writes=[("xt", b)])
            self.A("dve", lambda e, b=b, ss=ss: e.scalar_tensor_tensor(out=junk, in0=xt[b], scalar=1.0 / D, in1=xt[b],
                                                                       op0=ALU.mult, op1=ALU.mult, accum_out=ss),
                   reads=[("xt", b)], writes=[("junk",), ("ss", k)])
            self.A("act", lambda e, ss=ss, rs=rs: e.activation(out=rs, in_=ss, func=AF.Sqrt, bias=self.epsb[:, 0:1],
                                                               scale=1.0), reads=[("ss", k)], writes=[("rs", k)])
            self.A("dve", lambda e, rs=rs, rr=rr: e.reciprocal(out=rr, in_=rs), reads=[("rs", k)], writes=[("rr", k)])
            self.A("dve", lambda e, b=b, rr=rr: e.scalar_tensor_tensor(out=xt[b], in0=xt[b], scalar=rr, in1=gf,
                                                                       op0=ALU.mult, op1=ALU.mult),
                   reads=[("xt", b), ("rr", k), "gf"], writes=[("xt", b)])
            self.dma(self.out[tile * 128:(tile + 1) * 128, :], xt[b], reads=[("xt", b)], writes=[("H", tile)])

    def rope_store(self, psq, rows, dst, cosb, sinb, Rm, qb, t1, t2, rd_cs, wr_dst, pbank, rbank):
        pr = self.ps[rbank]
        k = self.nrope % 2
        self.nrope += 1
        qbk = self.qbs[k]
        qk = ("qb", k)
        self.A("act", lambda e: e.activation(out=qbk[rows, :], in_=psq[rows, :], func=AF.Copy),
               reads=[("ps", pbank)], writes=[qk])

        def part_b():
            self.mm(pr[rows, :], [(Rm[rows, rows], qbk[rows, :])], reads=[qk], writes=[("ps", rbank)])
            self.A("dve", lambda e: e.tensor_tensor(out=t1[rows, :], in0=psq[rows, :], in1=cosb[rows, :],
                                                    op=ALU.mult),
                   reads=[("ps", pbank)] + rd_cs, writes=[("t1",)])
            self.A("dve", lambda e: e.tensor_tensor(out=t2[rows, :], in0=pr[rows, :], in1=sinb[rows, :],
                                                    op=ALU.mult),
                   reads=[("ps", rbank)] + rd_cs, writes=[("t2",)])
            self.A("dve", lambda e: e.tensor_tensor(out=dst, in0=t1[rows, :], in1=t2[rows, :], op=ALU.add),
                   reads=[("t1",), ("t2",)], writes=wr_dst)
        prev = self.rope_pending
        self.rope_pending = part_b
        return prev

    def rope_flush(self):
        if self.rope_pending is not None:
            f = self.rope_pending
            self.rope_pending = None
            f()

    def load_v(self, vb, head, dil, rkey):
        per = NT // dil
        vdt = self.vd.tensor
        for r in range(dil):
            cps = max(1, per // 4) if dil == 1 else per
            for c0 in range(0, per, cps):
                off = self.vd.offset + ((r + dil * 128 * c0) * 8 + head) * 128
                src = bass.AP(vdt, off, [[dil * 1024, 128], [dil * 128 * 1024, cps], [1, 128]])
                self.dma(vb[:, r * per + c0:r * per + c0 + cps, :], src, reads=[("VD",)], writes=[rkey])

    def finalize(self, po, pbank, rows, dst, wr_dst, oa, k, sink_fn=None):
        n = dst.shape[-1] if False else None
        oak = ("oa", k)
        self.A("act", lambda e: e.activation(out=oa, in_=po, func=AF.Copy), reads=[("ps", pbank)], writes=[oak])
        if sink_fn is not None:
            sink_fn(oa, oak)
        self.A("dve", lambda e: e.reciprocal(out=oa[64:128, :], in_=oa[64:128, :]), reads=[oak], writes=[oak])
        sb = 6 + k
        psw = self.ps[sb]
        self.mm(psw[:, 0:oa.shape[1]], [(self.swapf, oa)], reads=[oak, "c_swap"], writes=[("ps", sb)])
        i0 = oa[rows, :]
        i1 = psw[rows, 0:oa.shape[1]]
        if len(dst.shape) == 3:
            i0 = i0.rearrange("p (j n) -> p j n", j=dst.shape[1])
            i1 = i1.rearrange("p (j n) -> p j n", j=dst.shape[1])
        self.A("dve", lambda e: e.tensor_tensor(out=dst, in0=i0, in1=i1, op=ALU.mult),
               reads=[oak, ("ps", sb)], writes=wr_dst)

    def fin1(self, po, pbank, oa, oak, sink_fn=None, eng="act"):
        if eng == "act":
            self.A("act", lambda e: e.activation(out=oa, in_=po, func=AF.Copy), reads=[("ps", pbank)], writes=[oak])
        else:
            self.A("dve", lambda e: e.tensor_copy(out=oa, in_=po), reads=[("ps", pbank)], writes=[oak])
        if sink_fn is not None:
            sink_fn(oa, oak)
        if eng == "act":
            self.A("act", lambda e: e.activation(out=oa[64:128, :], in_=oa[64:128, :], func=AF.Ln), reads=[oak],
                   writes=[oak])
            self.A("act", lambda e: e.activation(out=oa[64:128, :], in_=oa[64:128, :], func=AF.Exp, scale=-1.0),
                   reads=[oak], writes=[oak])
        else:
            self.A("dve", lambda e: e.reciprocal(out=oa[64:128, :], in_=oa[64:128, :]), reads=[oak], writes=[oak])

    def fin2(self, rows, dst, wr_dst, oa, oak, sb):
        psw = self.ps[sb]
        n = oa.shape[1]
        self.mm(psw[:, 0:n], [(self.swapf, oa)], reads=[oak, "c_swap"], writes=[("ps", sb)])
        i0 = oa[rows, :]
        i1 = psw[rows, 0:n]
        if len(dst.shape) == 3:
            i0 = i0.rearrange("p (j n) -> p j n", j=dst.shape[1])
            i1 = i1.rearrange("p (j n) -> p j n", j=dst.shape[1])
        self.A("dve", lambda e: e.tensor_tensor(out=dst, in0=i0, in1=i1, op=ALU.mult),
               reads=[oak, ("ps", sb)], writes=wr_dst)

    def run_pipe(self, items, L, D, G=1):
        pend = []
        n = len(items)
        total = n + L
        i = 0
        while i < total:
            hi = min(i + G, total)
            for a in range(i, hi):
                if a < n:
                    it = items[a]
                    if "pre" in it:
                        it["pre"]()
                    it["ab"]()
            for a in range(i, hi):
                jx = a - L
                if 0 <= jx < n:
                    it = items[jx]
                    it["c"]()
                    if "f1" in it:
                        it["f1"]()
                        pend.append((jx + D, it["f2"]))
                    while pend and pend[0][0] <= jx:
                        pend.pop(0)[1]()
            i = hi
        for _, f in pend:
            f()

    def out_proj(self, mixedT, w_rows, src):
        ar = self.ar
        wo = ar.alloc(8 * D, BF16).rearrange("p (c n) -> p c n", c=8)
        for c, parts in enumerate(w_rows):
            for (rows_ap, psl) in parts:
                self.dma(wo[psl, c, :], rows_ap, writes=[("wo", c, psl.start)], eng="pool")
        wkeys = [("wo", c, psl.start) for c, parts in enumerate(w_rows) for (_, psl) in parts]
        ht = [ar.alloc(D, F32) for _ in range(4)]
        for tile in range(NT):
            hb = tile % 4
            self.dma(ht[hb], src[tile * 128:(tile + 1) * 128, :], reads=[("H", tile)], writes=[("ht", hb)])
            for half in range(2):
                pb = 1 + (2 * tile + half) % 6
                po = self.ps[pb]
                hs = slice(half * 512, (half + 1) * 512)
                self.mm(po[:, :], [(mixedT[:, fc, tile * 128:(tile + 1) * 128], wo[:, fc, hs]) for fc in range(8)],
                        reads=[("mixedT",)] + wkeys, writes=[("ps", pb)])
                self.A("dve", lambda e, po=po, hb=hb, hs=hs: e.tensor_tensor(out=ht[hb][:, hs], in0=ht[hb][:, hs],
                                                                              in1=po[:, :], op=ALU.add),
                       reads=[("ps", pb), ("ht", hb)], writes=[("ht", hb)])
            self.dma(self.out[tile * 128:(tile + 1) * 128, :], ht[hb], reads=[("ht", hb)], writes=[("H", tile)])

    def alloc_proj_tmps(self, rope64=True):
        ar = self.ar
        if rope64:
            self.cosb = [ar.alloc(512, F32) for _ in range(2)]
            self.sinb = [ar.alloc(512, F32) for _ in range(2)]
        self.qb = ar.alloc(512, BF16)
        self.qbs = [self.qb, ar.alloc(512, BF16)]
        self.nrope = 0
        self.rope_pending = None
        self.t1 = ar.alloc(512, F32)
        self.t2 = ar.alloc(512, F32)
        self.vtok = [ar.alloc(8 * 128, BF16).rearrange("p (h d) -> p h d", h=8) for _ in range(2)]
        for b in range(2):
            vo = self.vtok[b][:, :, 64:128]
            self.A("pool", lambda e, vo=vo: e.memset(vo, 1.0), writes=[("vtok", b)])
        self.nv = 0

    def v_store(self, psv, pbank, nheads, tile):
        b = self.nv % 2
        self.nv += 1
        vt = self.vtok[b]
        self.A("act", lambda e: e.activation(out=vt[:, 0:nheads, 0:64], in_=psv, func=AF.Copy),
               reads=[("ps", pbank)], writes=[("vtok", b)])
        self.dma(self.vd[tile * 128:(tile + 1) * 128, 0:nheads, :], vt[:, 0:nheads, :], reads=[("vtok", b)],
                 writes=[("VD",)])

    def qkv_proj_even(self, i, j, col0, rope, qT, kT, src):
        ar = self.ar
        mark = ar.off
        wq = ar.alloc(8 * 512, BF16).rearrange("p (c n) -> p c n", c=8)
        wk = ar.alloc(8 * 512, BF16).rearrange("p (c n) -> p c n", c=8)
        wv = ar.alloc(8 * 512, BF16).rearrange("p (c n) -> p c n", c=8)
        win = self.even_w_in[j].rearrange("(c p) n -> p c n", p=128)
        for k, w in enumerate((wq, wk, wv)):
            self.dma(w, win[:, :, col0 + k * 512:col0 + (k + 1) * 512], writes=[("wqkv", k)], eng="pool")
        self.alloc_norm_bufs(2, 2)
        self.alloc_proj_tmps()
        g0 = 32 + 8 * i
        npj = 0
        self.norm_chunk(src, 0, g0, 0)
        for c in range(8):
            cs = slice(c * 512, (c + 1) * 512)
            hn = self.hnTs[c % 2]
            hk = ("hnT", c % 2)
            cb = c % 2
            if rope:
                self.dma(self.cosb[cb], self.rope64[0][:, cs], writes=[("cos", cb)])
                self.dma(self.sinb[cb], self.rope64[1][:, cs], writes=[("sin", cb)])
            for hp in range(4):
                for k, (w, dstT) in enumerate(((wq, qT), (wk, kT))):
                    pb = 1 + npj % 3
                    rb = 4 + npj % 2
                    npj += 1
                    pq = self.ps[pb]
                    self.mm(pq[:, :], [(w[:, dc, hp * 128:(hp + 1) * 128], hn[:, dc, :]) for dc in range(8)],
                            reads=[("wqkv", k), hk], writes=[("ps", pb)])
                    dst = dstT[:, hp, cs]
                    wr = [("qkT", k, hp)]
                    if rope and DBG != 11:
                        prev = self.rope_store(pq, slice(0, 128), dst, self.cosb[cb], self.sinb[cb], self.R64,
                                               self.qb, self.t1, self.t2, [("cos", cb), ("sin", cb)], wr, pb, rb)
                        if prev is not None:
                            prev()
                    else:
                        self.A("act", lambda e, pq=pq, dst=dst: e.activation(out=dst, in_=pq[:, :], func=AF.Copy),
                               reads=[("ps", pb)], writes=wr)
                if c + 1 < 8 and hp in (0, 2):
                    self.norm_chunk(src, c + 1, g0, (c + 1) % 2, tiles=(hp, hp + 1))
            if rope:
                self.rope_flush()
            for t in range(4):
                pb = 6 + t % 2
                pv = self.ps[pb]
                self.mm(pv[:, :], [(hn[:, dc, t * 128:(t + 1) * 128], wv[:, dc, :]) for dc in range(8)],
                        reads=[("wqkv", 2), hk], writes=[("ps", pb)])
                if DBG != 12:
                    self.v_store(pv[:, :].rearrange("p (h d) -> p h d", h=8), pb, 8, 4 * c + t)
        self.P.barrier()
        ar.off = mark

    def even_mixer(self, i):
        j = i // 2
        self.phase()
        ar = self.ar
        src = self.x if (SKIPFFN and i == 0) else self.out
        mixedT = ar.alloc(8 * S, BF16).rearrange("p (c n) -> p c n", c=8)
        qT = ar.alloc(4 * S, BF16).rearrange("p (c n) -> p c n", c=4)
        kT = ar.alloc(4 * S, BF16).rearrange("p (c n) -> p c n", c=4)
        self.qkv_proj_even(i, j, 0, True, qT, kT, src)
        if DBG in (1, 11, 12):
            return
        mark = ar.off
        vb = [ar.alloc(32 * 128, BF16).rearrange("p (c n) -> p c n", c=32) for _ in range(2)]
        accs = [ar.alloc(S, F32) for _ in range(2)]
        pt = [ar.alloc(512, BF16) for _ in range(8)]
        M3p = ar.alloc(512, BF16)
        for kk in range(2):
            self.A("pool", lambda e, kk=kk: e.tensor_copy(out=M3p[:, 256 * kk:256 * kk + 128], in_=self.Mle),
                   writes=[("M3",)])
            self.A("pool", lambda e, kk=kk: e.tensor_copy(out=M3p[:, 256 * kk + 128:256 * kk + 256], in_=self.Mge),
                   writes=[("M3",)])
        nit = 0
        nvb = 0
        qkr = lambda hp: [("qkT", 0, hp), ("qkT", 1, hp)]
        for h in range(8):
            hp, base = h // 2, (h % 2) * 64
            rows = slice(base, base + 64)
            items = []
            pres = []
            acc = accs[h % 2]
            ak = ("acc", h % 2)
            self.A("pool", lambda e, acc=acc: e.memset(acc, 0.0),
                   writes=[ak] + [("accf", h % 2, c) for c in range(8)])
            for bi, dil in enumerate((1, 4, 16)):
                vbb = vb[nvb % 2]
                vkey = ("vb", nvb % 2)
                nvb += 1
                pres.append((len(items), (lambda vbb=vbb, h=h, dil=dil, vkey=vkey: self.load_v(vbb, h, dil, vkey))))
                per = NT // dil
                sd = S // dil
                if per % 4 == 0:
                    groups = [(b, b + 2) for b4 in range(0, per, 4) for b in (b4, b4 + 1)]
                else:
                    groups = [(b,) for b in range(per)]
                for r in range(dil):
                    for grp_ in groups:
                        segs = []
                        col = 0
                        for b in grp_:
                            q0 = max(0, 128 * b - 64)
                            q1 = min(sd, 128 * b + 192)
                            segs.append((b, q0, q1 - q0, col))
                            col += q1 - q0
                        W = col
                        ms = 64 if grp_[0] == 0 else 0
                        if len(segs) == 2:
                            assert segs[0][1] + segs[0][2] == segs[1][1] and ms + segs[0][2] == 256
                        sbk = nit % 4
                        pk = nit % 8
                        ob = 4 + nit % 4
                        nit += 1
                        pS = self.ps[sbk]
                        po = self.ps[ob]
                        mms = [(strided(kT[rows, hp, :], (128 * b) * dil + r, dil, 128),
                                strided(qT[rows, hp, :], q0 * dil + r, dil, w), col_, w, vbb[:, r * per + b, :])
                               for (b, q0, w, col_) in segs]
                        msk = M3p[:, ms:ms + W]
                        ptk = pt[pk]
                        dsta = strided(acc, segs[0][1] * dil + r, dil, W)

                        def ab(pS=pS, mms=mms, W=W, sbk=sbk, pk=pk, msk=msk, ptk=ptk, hp=hp):
                            def smm(e):
                                for (kap, qap, col_, w, lv) in mms:
                                    ins = e.matmul(pS[:, col_:col_ + w], lhsT=kap, rhs=qap, start=True, stop=True)
                                return ins
                            self.A("pe", smm, reads=qkr(hp), writes=[("ps", sbk)])
                            self.A("act", lambda e: e.activation(out=ptk[:, 0:W], in_=pS[:, 0:W], func=AF.Exp,
                                                                 scale=0.125),
                                   reads=[("ps", sbk)], writes=[("pt", pk)])
                            self.A(MASK_ENG, lambda e: e.tensor_tensor(out=ptk[:, 0:W], in0=ptk[:, 0:W], in1=msk,
                                                                       op=ALU.mult),
                                   reads=[("pt", pk), ("M3",)], writes=[("pt", pk)])

                        def cc(po=po, mms=mms, W=W, ob=ob, pk=pk, ptk=ptk, vkey=vkey, dsta=dsta, ak=ak):
                            def pvm(e):
                                for (kap, qap, col_, w, lv) in mms:
                                    ins = e.matmul(po[:, col_:col_ + w], lhsT=lv, rhs=ptk[:, col_:col_ + w], start=True,
                                                   stop=True)
                                return ins
                            self.A("pe", pvm, reads=[vkey, ("pt", pk)], writes=[("ps", ob)])
                            self.A("dve", lambda e: e.tensor_tensor(out=dsta, in0=dsta, in1=po[:, 0:W], op=ALU.add),
                                   reads=[("ps", ob), ak], writes=[ak])
                        items.append(dict(ab=ab, c=cc))
            items[0]["pre"] = pres[0][1]
            items[pres[0][0] + 6]["pre"] = pres[1][1]
            items[pres[1][0] + 6]["pre"] = pres[2][1]
            self.run_pipe(items, 4, 0, 4)
            for c in range(8):
                cs = slice(c * 512, (c + 1) * 512)
                k = c % 2
                a = acc[:, cs]
                fk = ("accf", h % 2, c)
                self.A("act", lambda e, a=a: e.activation(out=a[64:128, :], in_=a[64:128, :], func=AF.Ln),
                       reads=[ak], writes=[fk])
                self.A("act", lambda e, a=a: e.activation(out=a[64:128, :], in_=a[64:128, :], func=AF.Exp, scale=-1.0),
                       reads=[fk], writes=[fk])
                psw = self.ps[6 + k]
                self.mm(psw[:, :], [(self.swapf, a)], reads=[fk, "c_swap"], writes=[("ps", 6 + k)])
                self.A("dve", lambda e, a=a, psw=psw, cs=cs, rows=rows, hp=hp: e.tensor_tensor(
                    out=mixedT[rows, hp, cs], in0=a[rows, :], in1=psw[rows, :], op=ALU.mult),
                    reads=[fk, ("ps", 6 + k)], writes=[("mixedT",)])
        self.P.barrier()
        ar.off = mark
        if DBG == 2:
            return
        self.qkv_proj_even(i, j, 1536, False, qT, kT, src)
        mark = ar.off
        vb = [ar.alloc(32 * 128, BF16).rearrange("p (c n) -> p c n", c=32) for _ in range(2)]
        Mi = ar.alloc(8 * 896, BF16).rearrange("p (h n) -> p h n", h=8)
        Mf = ar.alloc(8 * 896, BF16).rearrange("p (h n) -> p h n", h=8)
        raw = [ar.alloc(896, F32) for _ in range(2)]
        mk = [ar.alloc(896, F32) for _ in range(2)]
        pt = [ar.alloc(256, BF16) for _ in range(3)]
        oa = [ar.alloc(256, F32) for _ in range(2)]
        prev_start = 0
        for kind in range(2):
            self.dma(mk[kind], self.namask[kind], writes=[("mk", kind)])
        nr = 0
        for h in range(8):
            for kind, M in ((0, Mi), (1, Mf)):
                b = nr % 2
                nr += 1
                self.dma(raw[b], self.narpb[j, h, kind], writes=[("raw", b)])
                self.A("act", lambda e, b=b: e.activation(out=raw[b], in_=raw[b], func=AF.Exp), reads=[("raw", b)],
                       writes=[("raw", b)])
                self.A("dve", lambda e, b=b, M=M, h=h, kind=kind: e.tensor_tensor(out=M[:, h, :], in0=raw[b],
                                                                                  in1=mk[kind], op=ALU.mult),
                       reads=[("raw", b), ("mk", kind)], writes=[("M", kind, h)])
        pt += [ar.alloc(256, BF16) for _ in range(5)]
        items = []
        nit = 0
        grp = 0
        for h in range(8):
            hp, base = h // 2, (h % 2) * 64
            rows = slice(base, base + 64)
            vbb = vb[h % 2]
            vkey = ("vb", h % 2)
            pre = (lambda vbb=vbb, h=h, vkey=vkey: self.load_v(vbb, h, 1, vkey))
            hstart = len(items)
            for a in range(16):
                kind = 1 if a in (0, 15) else 0
                M = Mf if kind else Mi
                cl = list(range(max(0, 2 * a - 2), min(31, 2 * a + 3) + 1))
                ob = 4 + grp % 2
                k = grp % 2
                grp += 1
                po = self.ps[ob]
                qap = qT[rows, hp, 256 * a:256 * a + 256]
                for ci, cidx in enumerate(cl):
                    sbk = nit % 4
                    pk = nit % 8
                    nit += 1
                    pS = self.ps[sbk]
                    off = 4 * a - 2 * cidx
                    msl = M[:, h, (off + 6) * 64:(off + 10) * 64]
                    kap = kT[rows, hp, 128 * cidx:128 * cidx + 128]
                    lv = vbb[:, cidx, :]
                    ptk = pt[pk]

                    def ab(pS=pS, kap=kap, qap=qap, sbk=sbk, pk=pk, ptk=ptk, msl=msl, hp=hp, kind=kind, h=h):
                        self.mm(pS[:, 0:256], [(kap, qap)], reads=[("qkT", 0, hp), ("qkT", 1, hp)],
                                writes=[("ps", sbk)])
                        self.A("act", lambda e: e.activation(out=ptk, in_=pS[:, 0:256], func=AF.Exp, scale=0.125),
                               reads=[("ps", sbk)], writes=[("pt", pk)])
                        self.A("dve", lambda e: e.tensor_tensor(out=ptk, in0=ptk, in1=msl, op=ALU.mult),
                               reads=[("pt", pk), ("M", kind, h)], writes=[("pt", pk)])

                    def cc(po=po, lv=lv, ptk=ptk, pk=pk, ci=ci, nch=len(cl), vkey=vkey, ob=ob):
                        self.A("pe", lambda e: e.matmul(po[:, 0:256], lhsT=lv, rhs=ptk, start=(ci == 0),
                                                        stop=(ci == nch - 1)),
                               reads=[vkey, ("pt", pk)], writes=[("ps", ob)])
                    it = dict(ab=ab, c=cc)
                    if ci == len(cl) - 1:
                        dst = mixedT[rows, 4 + hp, 256 * a:256 * a + 256]
                        it["f1"] = (lambda po=po, ob=ob, k=k: self.fin1(po[:, 0:256], ob, oa[k], ("oa", k)))
                        it["f2"] = (lambda rows=rows, dst=dst, k=k: self.fin2(rows, dst, [("mixedT",)], oa[k],
                                                                              ("oa", k), 6 + k))
                    items.append(it)
            if h == 0:
                items[0]["pre"] = pre
            else:
                items[prev_start + 12]["pre"] = pre
            prev_start = hstart
        self.run_pipe(items, 4, 4, 4)
        self.P.barrier()
        ar.off = mark
        if DBG == 3:
            return
        wo = self.even_w_out[j]
        w_rows = [[(wo[c * 128:(c + 1) * 128, :], slice(0, 128))] for c in range(8)]
        self.out_proj(mixedT, w_rows, src)

    def odd_mixer(self, i):
        j = i // 2
        self.phase()
        ar = self.ar
        src = self.out
        qaT = ar.alloc(3 * S, BF16).rearrange("p (c n) -> p c n", c=3)
        kvaT = ar.alloc(2 * S, BF16).rearrange("p (c n) -> p c n", c=2)
        kTh = [ar.alloc(S, BF16) for _ in range(2)]
        mark_mla = ar.off
        qT = ar.alloc(4 * S, BF16).rearrange("p (c n) -> p c n", c=4)
        kT = ar.alloc(S, BF16)
        mark = ar.off
        wb = ar.alloc(8 * 1440, BF16).rearrange("p (c n) -> p c n", c=8)
        win = self.odd_w_in[j].rearrange("(c p) n -> p c n", p=128)
        for two in range(2):
            for jq in range(4):
                self.dma(wb[:, :, jq * 128 + two * 64:jq * 128 + two * 64 + 64],
                         win[:, :, two * 256 + jq * 64:two * 256 + jq * 64 + 64], writes=[("wb", 2 + two, jq)],
                         eng="pool")
        self.dma(wb[:, :, 512:768], win[:, :, 512:768], writes=[("wb", 0)], eng="pool")
        self.dma(wb[:, :, 768:1440], win[:, :, 768:1440], writes=[("wb", 1)], eng="pool")
        wbk = [("wb", 0), ("wb", 1)] + [("wb", 2 + two, jq) for two in range(2) for jq in range(4)]
        self.alloc_norm_bufs(2, 2)
        self.alloc_proj_tmps()
        c32 = [ar.alloc(512, F32) for _ in range(2)]
        s32 = [ar.alloc(512, F32) for _ in range(2)]
        qas = [ar.alloc(640, BF16) for _ in range(2)]
        g0 = 32 + 8 * i
        npj = 0
        nq = 0
        psT = self.ps[0].bitcast(BF16)
        self.norm_chunk(src, 0, g0, 0)
        for c in range(8):
            cs = slice(c * 512, (c + 1) * 512)
            hn = self.hnTs[c % 2]
            hk = ("hnT", c % 2)
            cb = c % 2
            self.dma(self.cosb[cb], self.rope64[0][:, cs], writes=[("cos", cb)])
            self.dma(self.sinb[cb], self.rope64[1][:, cs], writes=[("sin", cb)])
            self.dma(c32[cb][64:96, :], self.rope32[0][64:96, cs], writes=[("c32", cb)])
            self.dma(s32[cb][64:96, :], self.rope32[1][64:96, cs], writes=[("s32", cb)])
            for jq in range(5):
                pb = 1 + npj % 3
                rb = 4 + npj % 2
                npj += 1
                pq = self.ps[pb]
                if jq < 4:
                    def wsl(dc, jq=jq):
                        return wb[:, dc, 128 * jq:128 * jq + 128]
                    dst = qT[:, jq, cs]
                    wr = [("qT", jq)]
                else:
                    def wsl(dc):
                        return wb[:, dc, 512:640]
                    dst = kT[:, cs]
                    wr = [("kT",)]
                self.mm(pq[:, :], [(wsl(dc), hn[:, dc, :]) for dc in range(8)], reads=wbk + [hk],
                        writes=[("ps", pb)])
                prev = self.rope_store(pq, slice(0, 128), dst, self.cosb[cb], self.sinb[cb], self.R64, self.qb,
                                       self.t1, self.t2, [("cos", cb), ("sin", cb)], wr, pb, rb)
                if prev is not None:
                    prev()
                if c + 1 < 8 and jq in (1, 3):
                    self.norm_chunk(src, c + 1, g0, (c + 1) % 2, tiles=(jq - 1, jq))
            pb = 1 + npj % 3
            rb = 4 + npj % 2
            npj += 1
            pq = self.ps[pb]
            r32 = slice(64, 96)
            self.mm(pq[r32, :], [(wb[:, dc, 1408:1440], hn[:, dc, :]) for dc in range(8)],
                    reads=wbk + [hk], writes=[("ps", pb)])
            prev = self.rope_store(pq, r32, kTh[0][r32, cs], c32[cb], s32[cb], self.R32, self.qb, self.t1, self.t2,
                                   [("c32", cb), ("s32", cb)], [("kTh", 0)], pb, rb)
            if prev is not None:
                prev()
            self.rope_flush()
            self.A("pool", lambda e, cs=cs: e.tensor_copy(out=kTh[1][r32, cs], in_=kTh[0][r32, cs]),
                   reads=[("kTh", 0)], writes=[("kTh", 1)])
            for t in range(4):
                tile = 4 * c + t
                ts = slice(t * 128, (t + 1) * 128)
                pb = 6 + t % 2
                pv = self.ps[pb]
                self.mm(pv[:, 0:128], [(hn[:, dc, ts], wb[:, dc, 640:768]) for dc in range(8)],
                        reads=wbk + [hk], writes=[("ps", pb)])
                self.v_store(pv[:, 0:128].rearrange("p (h d) -> p h d", h=2), pb, 2, tile)
                pb = 1 + npj % 3
                npj += 1
                pa = self.ps[pb]
                self.mm(pa[:, :], [(hn[:, dc, ts], wb[:, dc, 768:1280]) for dc in range(8)],
                        reads=wbk + [hk], writes=[("ps", pb)])
                pb2 = 1 + npj % 3
                npj += 1
                pa2 = self.ps[pb2]
                self.mm(pa2[:, 0:128], [(hn[:, dc, ts], wb[:, dc, 1280:1408]) for dc in range(8)],
                        reads=wbk + [hk], writes=[("ps", pb2)])
                qs = qas[nq % 2]
                qk = ("qas", nq % 2)
                nq += 1
                k = self.statk % 16
                self.statk += 1
                ss, rs, rr = self.stat[:, k:k + 1], self.stat[:, 16 + k:17 + k], self.stat[:, 32 + k:33 + k]
                k2 = self.statk % 16
                self.statk += 1
                ss2, ss3, rs2, rr2 = (self.stat[:, k2:k2 + 1], self.stat[:, 48 + k2 % 8:49 + k2 % 8],
                                      self.stat[:, 16 + k2:17 + k2], self.stat[:, 32 + k2:33 + k2])
                jk = self.junk
                self.A("act", lambda e, pa=pa, ss=ss: e.activation(out=jk[:, 0:384], in_=pa[:, 0:384], func=AF.Square,
                                                                   scale=float(384 ** -0.5), accum_out=ss),
                       reads=[("ps", pb)], writes=[("junk",), ("ss", k)])
                self.A("act", lambda e, ss=ss, rs=rs: e.activation(out=rs, in_=ss, func=AF.Sqrt,
                                                                   bias=self.epsb[:, 0:1], scale=1.0),
                       reads=[("ss", k)], writes=[("rs", k)])
                self.A("dve", lambda e, rs=rs, rr=rr: e.reciprocal(out=rr, in_=rs), reads=[("rs", k)],
                       writes=[("rr", k)])
                self.A("dve", lambda e, pa=pa, qs=qs, rr=rr: e.tensor_scalar(out=qs[:, 0:384], in0=pa[:, 0:384],
                                                                             scalar1=rr, scalar2=None, op0=ALU.mult),
                       reads=[("ps", pb), ("rr", k)], writes=[qk])
                self.A("act", lambda e, pa=pa, ss2=ss2: e.activation(out=jk[:, 384:512], in_=pa[:, 384:512],
                                                                     func=AF.Square, scale=1.0 / 16, accum_out=ss2),
                       reads=[("ps", pb)], writes=[("junk",), ("ss", k2)])
                self.A("act", lambda e, pa2=pa2, ss3=ss3: e.activation(out=jk[:, 512:640], in_=pa2[:, 0:128],
                                                                       func=AF.Square, scale=1.0 / 16,
                                                                       accum_out=ss3),
                       reads=[("ps", pb2)], writes=[("junk",), ("ss3", k2 % 8)])
                self.A("dve", lambda e, ss2=ss2, ss3=ss3: e.tensor_tensor(out=ss2, in0=ss2, in1=ss3, op=ALU.add),
                       reads=[("ss", k2), ("ss3", k2 % 8)], writes=[("ss", k2)])
                self.A("act", lambda e, ss2=ss2, rs2=rs2: e.activation(out=rs2, in_=ss2, func=AF.Sqrt,
                                                                       bias=self.epsb[:, 0:1], scale=1.0),
                       reads=[("ss", k2)], writes=[("rs", k2)])
                self.A("dve", lambda e, rs2=rs2, rr2=rr2: e.reciprocal(out=rr2, in_=rs2), reads=[("rs", k2)],
                       writes=[("rr", k2)])
                self.A("dve", lambda e, pa=pa, qs=qs, rr2=rr2: e.tensor_scalar(
                    out=qs[:, 384:512], in0=pa[:, 384:512], scalar1=rr2, scalar2=None, op0=ALU.mult),
                    reads=[("ps", pb), ("rr", k2)], writes=[qk])
                self.A("dve", lambda e, pa2=pa2, qs=qs, rr2=rr2: e.tensor_scalar(
                    out=qs[:, 512:640], in0=pa2[:, 0:128], scalar1=rr2, scalar2=None, op0=ALU.mult),
                    reads=[("ps", pb2), ("rr", k2)], writes=[qk])

                def tr(e, qs=qs):
                    for cc in range(5):
                        ins = e.transpose(out=psT[:, cc * 128:(cc + 1) * 128], in_=qs[:, cc * 128:(cc + 1) * 128],
                                          identity=self.ident)
                    return ins
                self.A("pe", tr, reads=[qk], writes=[("ps", 0)])
                gq = bc_last(self.mg[:, 5 * j:5 * j + 3], 128)
                gk = bc_last(self.mg[:, 5 * j + 3:5 * j + 5], 128)
                cols = slice(tile * 128, (tile + 1) * 128)
                self.A("dve", lambda e, gq=gq, cols=cols: e.tensor_tensor(
                    out=qaT[:, :, cols], in0=psT[:, 0:384].rearrange("p (c n) -> p c n", c=3), in1=gq, op=ALU.mult),
                    reads=[("ps", 0), "c_mg"], writes=[("qaT",)])
                self.A("dve", lambda e, gk=gk, cols=cols: e.tensor_tensor(
                    out=kvaT[:, :, cols], in0=psT[:, 384:640].rearrange("p (c n) -> p c n", c=2), in1=gk,
                    op=ALU.mult), reads=[("ps", 0), "c_mg"], writes=[("kvaT",)])
        self.P.barrier()
        ar.off = mark
        mixedT = ar.alloc(8 * S, BF16).rearrange("p (c n) -> p c n", c=8)
        mark2 = ar.off
        vb = [ar.alloc(32 * 128, BF16).rearrange("p (c n) -> p c n", c=32) for _ in range(2)]
        pt = [ar.alloc(512, BF16) for _ in range(8)]
        oa = [ar.alloc(512, F32) for _ in range(4)]
        items = []
        nit = 0
        grp = 0
        qrd = [("qT", q) for q in range(4)] + [("kT",)]
        for g in range(2):
            rows = slice(64 * g, 64 * g + 64)
            vbb = vb[g]
            vkey = ("vb", g)
            self.load_v(vbb, g, 1, vkey)
            for qb_ in range(NT):
                cl = [c for c in (qb_ - 1, qb_, qb_ + 1) if 0 <= c < NT]
                ob = 4 + grp % 2
                k = grp % 4
                sb = 6 + grp % 2
                grp += 1
                po = self.ps[ob]
                qap = qT[rows, :, 128 * qb_:128 * qb_ + 128]
                for ci, cidx in enumerate(cl):
                    sbk = nit % 4
                    pk = nit % 8
                    nit += 1
                    pS = self.ps[sbk]
                    ptk = pt[pk]
                    kap = kT[rows, 128 * cidx:128 * cidx + 128]
                    msk = None if cidx == qb_ else (self.Mge if cidx < qb_ else self.Mle)
                    lv = vbb[:, cidx, :]

                    def ab(pS=pS, kap=kap, qap=qap, sbk=sbk, pk=pk, ptk=ptk, msk=msk):
                        self.mm(pS[:, :].rearrange("p (j n) -> p j n", j=4), [(kap, qap)], reads=qrd,
                                writes=[("ps", sbk)])
                        self.A("act", lambda e: e.activation(out=ptk, in_=pS[:, :], func=AF.Exp, scale=0.125),
                               reads=[("ps", sbk)], writes=[("pt", pk)])
                        if msk is not None:
                            self.A("dve", lambda e: e.tensor_tensor(
                                out=ptk.rearrange("p (j n) -> p j n", j=4),
                                in0=ptk.rearrange("p (j n) -> p j n", j=4), in1=bc_mid(msk, 4), op=ALU.mult),
                                reads=[("pt", pk)], writes=[("pt", pk)])

                    def cc(po=po, lv=lv, ptk=ptk, pk=pk, ci=ci, nch=len(cl), vkey=vkey, ob=ob):
                        self.A("pe", lambda e: e.matmul(po[:, :], lhsT=lv, rhs=ptk, start=(ci == 0),
                                                        stop=(ci == nch - 1)),
                               reads=[vkey, ("pt", pk)], writes=[("ps", ob)])
                    it = dict(ab=ab, c=cc)
                    if ci == len(cl) - 1:
                        def sink_fn(oa_, oak, g=g):
                            sk = bc_last(self.esink[64:128, 8 * j + 4 * g:8 * j + 4 * g + 4], 128)
                            self.A("dve", lambda e: e.tensor_tensor(
                                out=oa_[64:128, :].rearrange("p (j n) -> p j n", j=4),
                                in0=oa_[64:128, :].rearrange("p (j n) -> p j n", j=4), in1=sk, op=ALU.add),
                                reads=[oak, "c_sink"], writes=[oak])
                        dst = mixedT[rows, 0:4, 128 * qb_:128 * qb_ + 128]
                        it["f1"] = (lambda po=po, ob=ob, k=k, sink_fn=sink_fn: self.fin1(
                            po[:, :], ob, oa[k], ("oa", k), sink_fn=sink_fn))
                        it["f2"] = (lambda rows=rows, dst=dst, k=k, sb=sb: self.fin2(
                            rows, dst, [("mixedT",)], oa[k], ("oa", k), sb))
                    items.append(it)
        self.run_pipe(items, 4, 6, 4)
        self.P.barrier()
        ar.off = mark_mla
        wqb = ar.alloc(3 * 768, BF16).rearrange("p (c n) -> p c n", c=3)
        wkvb = ar.alloc(2 * 1024, BF16).rearrange("p (c n) -> p c n", c=2)
        self.dma(wqb, self.w_qb[j].rearrange("(c p) n -> p c n", p=128), writes=[("wqb",)], eng="pool")
        self.dma(wkvb, self.w_kvb[j].rearrange("(c p) n -> p c n", p=128), writes=[("wkvb",)], eng="pool")
        c32 = [ar.alloc(512, F32) for _ in range(2)]
        s32 = [ar.alloc(512, F32) for _ in range(2)]
        qTh = [ar.alloc(S, BF16) for _ in range(2)]
        vb = [ar.alloc(24 * 128, BF16).rearrange("p (c n) -> p c n", c=24) for _ in range(0)]
        assert ar.off <= mark, (ar.off, mark)
        ar.off = mark2
        self.alloc_proj_tmps(rope64=False)
        vb = [ar.alloc(32 * 128, BF16).rearrange("p (c n) -> p c n", c=32) for _ in range(2)]
        pt = [ar.alloc(512, BF16) for _ in range(8)]
        oa = [ar.alloc(512, F32) for _ in range(2)]
        for tile in range(NT):
            pb = 6 + tile % 2
            pv_ = self.ps[pb]
            ts = slice(tile * 128, (tile + 1) * 128)

            def vsl(c):
                a = wkvb[:, c, 64:128]
                return bass.AP(a.tensor, a.offset, [list(a.ap[0]), [128, 8], [1, 64]])
            self.mm(pv_[:, :].rearrange("p (h d) -> p h d", h=8), [(kvaT[:, c, ts], vsl(c)) for c in range(2)],
                    reads=[("wkvb",)], writes=[("ps", pb)])
            self.v_store(pv_[:, :].rearrange("p (h d) -> p h d", h=8), pb, 8, tile)
        r32 = slice(64, 96)
        sc = float(96 ** -0.5)
        self.npj = 0

        def head_pre(h):
            hb = h % 2
            self.load_v(vb[hb], h, 1, ("vb", hb))
            for c in range(8):
                cs = slice(c * 512, (c + 1) * 512)
                cb = (8 * h + c) % 2
                self.dma(c32[cb][r32, :], self.rope32[0][r32, cs], writes=[("c32", cb)])
                self.dma(s32[cb][r32, :], self.rope32[1][r32, cs], writes=[("s32", cb)])
                pb = 1 + self.npj % 3
                rb = 6 + self.npj % 2
                self.npj += 1
                pq = self.ps[pb]
                self.mm(pq[0:96, :], [(wqb[:, cq, 96 * h:96 * h + 96], qaT[:, cq, cs]) for cq in range(3)],
                        reads=[("wqb",)], writes=[("ps", pb)])
                self.A("dve", lambda e, pq=pq, hb=hb, cs=cs: e.tensor_copy(out=qTh[hb][0:64, cs], in_=pq[0:64, :]),
                       reads=[("ps", pb)], writes=[("qTh", hb)])
                prev = self.rope_store(pq, r32, qTh[hb][r32, cs], c32[cb], s32[cb], self.R32, self.qb, self.t1,
                                       self.t2, [("c32", cb), ("s32", cb)], [("qTh", hb)], pb, rb)
                if prev is not None:
                    prev()
                pb = 1 + self.npj % 3
                self.npj += 1
                pk_ = self.ps[pb]
                self.mm(pk_[0:64, :], [(wkvb[:, cq, 128 * h:128 * h + 64], kvaT[:, cq, cs]) for cq in range(2)],
                        reads=[("wkvb",)], writes=[("ps", pb)])
                self.A("dve", lambda e, pk_=pk_, hb=hb, cs=cs: e.tensor_copy(out=kTh[hb][0:64, cs], in_=pk_[0:64, :]),
                       reads=[("ps", pb)], writes=[("kThn", hb)])
            self.rope_flush()

        items = []
        nit = 0
        grp = 0
        for h in range(8):
            hb = h % 2
            base = (h % 2) * 64
            rows = slice(base, base + 64)
            vbb = vb[hb]
            vkey = ("vb", hb)
            for qc in range(8):
                qs_ = slice(qc * 512, (qc + 1) * 512)
                ob = 4 + grp % 2
                k = grp % 2
                grp += 1
                po = self.ps[ob]
                for kc in range(NT):
                    sbk = nit % 4
                    pk = nit % 8
                    nit += 1
                    pS = self.ps[sbk]
                    ptk = pt[pk]
                    kap = kTh[hb][0:96, 128 * kc:128 * kc + 128]
                    qap = qTh[hb][0:96, qs_]
                    lv = vbb[:, kc, :]

                    def ab(pS=pS, kap=kap, qap=qap, sbk=sbk, pk=pk, ptk=ptk, hb=hb):
                        self.mm(pS[:, :], [(kap, qap)], reads=[("qTh", hb), ("kThn", hb)], writes=[("ps", sbk)])
                        self.A("act", lambda e: e.activation(out=ptk, in_=pS[:, :], func=AF.Exp, scale=sc),
                               reads=[("ps", sbk)], writes=[("pt", pk)])

                    def cc(po=po, lv=lv, ptk=ptk, pk=pk, kc=kc, vkey=vkey, ob=ob):
                        self.A("pe", lambda e: e.matmul(po[:, :], lhsT=lv, rhs=ptk, start=(kc == 0),
                                                        stop=(kc == NT - 1)),
                               reads=[vkey, ("pt", pk)], writes=[("ps", ob)])
                    it = dict(ab=ab, c=cc)
                    if kc == NT - 1:
                        dst = mixedT[rows, 4 + h // 2, qs_]
                        it["f1"] = (lambda po=po, ob=ob, k=k: self.fin1(po[:, :], ob, oa[k], ("oa", k), eng="dve"))
                        it["f2"] = (lambda rows=rows, dst=dst, k=k: self.fin2(rows, dst, [("mixedT",)], oa[k],
                                                                              ("oa", k), 6 + k))
                    if qc == 1 and kc == 0 and h + 1 < 8:
                        it["pre"] = (lambda h=h: head_pre(h + 1))
                    items.append(it)
        head_pre(0)
        self.run_pipe(items, 5, 12)
        self.P.barrier()
        ar.off = mark2
        wo = self.odd_w_out[j]
        w_rows = []
        for c in range(4):
            w_rows.append([(wo[64 * c:64 * c + 64, :], slice(0, 64)),
                           (wo[64 * (c + 4):64 * (c + 4) + 64, :], slice(64, 128))])
        for c in range(4, 8):
            w_rows.append([(wo[c * 128:(c + 1) * 128, :], slice(0, 128))])
        self.out_proj(mixedT, w_rows, src)


def _rope_tab(dim, nrows_fn):
    pos = np.arange(S, dtype=np.float32)
    inv = (np.float32(10000.0) ** (-np.arange(0, dim, 2, dtype=np.float32) / np.float32(dim))).astype(np.float32)
    ang = pos[None, :] * inv[:, None]
    return np.cos(ang).astype(np.float32), np.sin(ang).astype(np.float32)


def _constants():
    ident = np.eye(128, dtype=np.float32)
    swap = np.zeros((128, 128), np.float32)
    R64 = np.zeros((128, 128), np.float32)
    R32 = np.zeros((128, 128), np.float32)
    for m in range(128):
        swap[(m + 64) % 128, m] = 1.0
        jj = m % 64
        blk = m - jj
        if jj < 32:
            R64[blk + jj + 32, m] = -1.0
        else:
            R64[blk + jj - 32, m] = 1.0
    for m in range(64, 96):
        jj = m - 64
        if jj < 16:
            R32[m + 16, m] = -1.0
        else:
            R32[m - 16, m] = 1.0
    kk = np.arange(128)[:, None]
    qq = np.arange(128)[None, :]
    Mge = (kk >= qq).astype(np.float32)
    Mle = (kk <= qq).astype(np.float32)
    cst = np.stack([ident, swap, R64, R32, Mge, Mle]).astype(np.float32)
    c64, s64 = _rope_tab(64, None)
    rope64 = np.zeros((2, 128, S), np.float32)
    for p in range(128):
        rope64[0, p] = c64[p % 32]
        rope64[1, p] = s64[p % 32]
    c32, s32 = _rope_tab(32, None)
    rope32 = np.zeros((2, 128, S), np.float32)
    for p in range(64, 96):
        rope32[0, p] = c32[(p - 64) % 16]
        rope32[1, p] = s32[(p - 64) % 16]
    kr = np.arange(128) // 64
    kc = np.arange(128) % 64
    u = np.arange(896) // 64 - 6
    qc = np.arange(896) % 64
    dr = kr[:, None] - u[None, :]
    dc = kc[:, None] - qc[None, :]
    ws = np.clip(qc - 8, 0, 48)
    colv = (kc[:, None] >= ws[None, :]) & (kc[:, None] < ws[None, :] + 16)
    rowv_int = (dr >= -4) & (dr <= 3)
    rowv_full = (dr >= -7) & (dr <= 7)
    namask = np.stack([(colv & rowv_int), (colv & rowv_full)]).astype(np.float32)
    dri = np.clip(dr + 7, 0, 14)
    dci = np.clip(dc + 15, 0, 30)
    return cst, rope64, rope32, namask, dri, dci


_CACHE = {}


def _get_nc(nsteps=12, final=True):
    key = (nsteps, final)
    if key not in _CACHE:
        _CACHE[key] = Builder(nsteps, final).build()
    return _CACHE[key]


def _prep(inputs):
    f = lambda a: np.ascontiguousarray(np.asarray(a, dtype=np.float32))
    cst, rope64, rope32, namask, dri, dci = _constants()
    gains = np.zeros((128, 96), np.float32)
    for blk, name in enumerate(("ffn1_norm", "mix_norm", "ffn2_norm")):
        g = f(inputs[name])
        for i in range(4):
            gains[:, blk * 32 + 8 * i:blk * 32 + 8 * i + 8] = g[i].reshape(8, 128).T
    mla_g = np.zeros((128, 10), np.float32)
    qn, kn = f(inputs["mla_q_norm"]), f(inputs["mla_kv_norm"])
    for j in range(2):
        mla_g[:, 5 * j:5 * j + 3] = qn[j].reshape(3, 128).T
        mla_g[:, 5 * j + 3:5 * j + 5] = kn[j].reshape(2, 128).T
    rpb = f(inputs["na_rel_bias"])
    g = rpb[:, :, dri, dci]
    narpb = np.ascontiguousarray(np.stack([g, g], axis=2))
    shared = dict(
        gains=gains, fnorm=f(inputs["final_norm"]).reshape(1, D),
        ffn1_w1=f(inputs["ffn1_w1"]), ffn1_w3=f(inputs["ffn1_w3"]), ffn1_w2=f(inputs["ffn1_w2"]),
        ffn2_w1=f(inputs["ffn2_w1"]), ffn2_w3=f(inputs["ffn2_w3"]), ffn2_w2=f(inputs["ffn2_w2"]),
        even_w_in=f(inputs["even_w_in"]), even_w_out=f(inputs["even_w_out"]),
        odd_w_in=f(inputs["odd_w_in"]), odd_w_out=f(inputs["odd_w_out"]),
        mla_w_qb=f(inputs["mla_w_qb"]), mla_w_kvb=f(inputs["mla_w_kvb"]), mla_g=mla_g,
        sink=f(inputs["swa_sink"]).reshape(1, 16), narpb=narpb, namask=namask, cst=cst,
        rope64=rope64, rope32=rope32)
    return shared


def kernel(**inputs):
    nsteps = inputs.pop("_nsteps", 12)
    final = inputs.pop("_final", True)
    cores = inputs.pop("_cores", list(range(8)))
    nc = _get_nc(nsteps, final)
    shared = _prep(inputs)
    x = np.asarray(inputs["x"], dtype=np.float32)
    in_maps = []
    for b in cores:
        m = dict(shared)
        m["x"] = np.ascontiguousarray(x[b])
        in_maps.append(m)
    res = run_bass_kernel_spmd(nc, in_maps, core_ids=list(range(len(cores))))
    return np.stack([np.asarray(r["out"], dtype=np.float32) for r in res.results], axis=0)
```

```python
import numpy as np
import ml_dtypes
from contextlib import ExitStack
import concourse.bass as bass
import concourse.mybir as mybir
from concourse.bass_utils import run_bass_kernel_spmd

F32 = mybir.dt.float32
BF16 = mybir.dt.bfloat16
AF = mybir.ActivationFunctionType
ALU = mybir.AluOpType

import os
DBG = int(os.environ.get("KDBG", "0"))
MASK_ENG = os.environ.get("KMASKENG", "dve")
SKIP_SAME = int(os.environ.get("KSKIPSAME", "0"))
SKIPFFN = int(os.environ.get("KSKIPFFN", "0"))
DBG2 = int(os.environ.get("KDBG2", "0"))
S = 4096
D = 1024
DFF = 2816
NT = S // 128
ARENA_KIB = 204


class Op:
    __slots__ = ("eng", "fn", "dma", "idx", "deps", "signal", "sem", "val", "ringdep", "used")


class Prog:
    ENGS = ("pe", "act", "dve", "pool", "sp")
    NRING = 8

    def __init__(self, nc):
        self.nc = nc
        self.ops = []
        self.lastw = {}
        self.readers = {}
        self.eng_ops = {e: [] for e in self.ENGS}
        self.ndma = {e: 0 for e in self.ENGS}
        self.dma_ops = {e: [] for e in self.ENGS}
        self.pending_bar = {e: [] for e in self.ENGS}

    def add(self, eng, fn, reads=(), writes=(), dma=False):
        op = Op()
        op.eng, op.fn, op.dma, op.idx = eng, fn, dma, len(self.ops)
        op.used = False
        px = [r for r in reads if isinstance(r, tuple) and r[0] == "ps"]
        if px:
            reads = [r for r in reads if not (isinstance(r, tuple) and r[0] == "ps")]
            writes = list(writes) + [r for r in px if r not in writes]
        deps = {}
        for r in reads:
            w = self.lastw.get(r)
            if w is not None:
                deps[w] = "raw"
        for r in writes:
            w = self.lastw.get(r)
            if w is not None:
                deps.setdefault(w, "waw")
            for rd in self.readers.get(r, ()):
                deps.setdefault(rd, "war")
        op.deps = []
        for d, kind in deps.items():
            o = self.ops[d]
            if SKIP_SAME and o.eng == eng and (not o.dma) and (not dma) and kind != "raw":
                continue
            op.deps.append(d)
            o.used = True
        if self.pending_bar[eng]:
            for d in self.pending_bar[eng]:
                if d not in op.deps:
                    op.deps.append(d)
            self.pending_bar[eng] = []
        op.ringdep = None
        if dma:
            k = self.ndma[eng]
            self.ndma[eng] = k + 1
            if k >= self.NRING:
                op.ringdep = self.dma_ops[eng][k - self.NRING]
            self.dma_ops[eng].append(op)
        for r in reads:
            self.readers.setdefault(r, []).append(op.idx)
        for r in writes:
            self.lastw[r] = op.idx
            self.readers[r] = []
        self.ops.append(op)
        self.eng_ops[eng].append(op)
        return op

    def barrier(self):
        last = [self.eng_ops[e][-1] for e in self.ENGS if self.eng_ops[e]]
        pend = []
        for e in self.ENGS:
            pend += self.dma_ops[e][-self.NRING:]
        deps = []
        for o in last + pend:
            o.used = True
            if o.idx not in deps:
                deps.append(o.idx)
        self.pending_bar = {e: list(deps) for e in self.ENGS}
        self.lastw = {}
        self.readers = {}

    def emit(self, stack):
        nc = self.nc
        esem = {e: stack.enter_context(nc.semaphore("s_" + e)) for e in self.ENGS}
        rings = {e: [stack.enter_context(nc.semaphore("r_%s%d" % (e, i))) for i in range(self.NRING)]
                 for e in self.ENGS if self.ndma[e]}
        cnt = {e: 0 for e in self.ENGS}
        dcnt = {e: 0 for e in self.ENGS}
        for op in self.ops:
            if op.dma:
                k = dcnt[op.eng]
                dcnt[op.eng] = k + 1
                op.sem = rings[op.eng][k % self.NRING]
                op.val = 16 * (k // self.NRING + 1)
                op.signal = True
            else:
                op.signal = op.used
                if op.signal:
                    cnt[op.eng] += 1
                op.sem = esem[op.eng]
                op.val = cnt[op.eng]
        ops = self.ops
        block = stack.enter_context(nc.Block())
        deco = {"pe": block.tensor, "act": block.scalar, "dve": block.vector,
                "pool": block.gpsimd, "sp": block.sync}

        def run(eng):
            def body(e):
                waited = {}
                for op in self.eng_ops[eng]:
                    needs = {}
                    lst = [ops[d] for d in op.deps]
                    if op.ringdep is not None:
                        lst.append(op.ringdep)
                    for o in lst:
                        if needs.get(id(o.sem), (None, 0))[1] < o.val:
                            needs[id(o.sem)] = (o.sem, o.val)
                    for sid, (sem, val) in needs.items():
                        if waited.get(sid, 0) >= val:
                            continue
                        e.wait_ge(sem, val)
                        waited[sid] = val
                    ins = op.fn(e)
                    if op.signal:
                        ins.then_inc(op.sem, 16 if op.dma else 1)
                if eng == "sp":
                    for en in self.ENGS:
                        for o in self.dma_ops[en][-self.NRING:]:
                            if waited.get(id(o.sem), 0) < o.val:
                                e.wait_ge(o.sem, o.val)
                                waited[id(o.sem)] = o.val
            deco[eng](body)

        for eng in self.ENGS:
            run(eng)


class Arena:
    def __init__(self, t16):
        self.t16 = t16
        self.t32 = t16.bitcast(F32)
        self.off = 0
        self.cap = ARENA_KIB * 1024

    def alloc(self, n, dt):
        sz = 2 if dt == BF16 else 4
        self.off = (self.off + 63) // 64 * 64
        o = self.off
        self.off += n * sz
        assert self.off <= self.cap, ("arena overflow", self.off)
        if dt == BF16:
            return self.t16[:, o // 2:o // 2 + n]
        return self.t32[:, o // 4:o // 4 + n]


def bc_last(ap, n):
    return bass.AP(ap.tensor, ap.offset, [list(p) for p in ap.ap] + [[0, n]])


def bc_mid(ap, n):
    a = [list(p) for p in ap.ap]
    return bass.AP(ap.tensor, ap.offset, [a[0], [0, n]] + a[1:])


def strided(ap2, start, step, cnt):
    a = [list(p) for p in ap2.ap]
    assert len(a) == 2 and a[1][0] == 1
    return bass.AP(ap2.tensor, ap2.offset + start, [a[0], [step, cnt]])


class Builder:
    def __init__(self, nsteps=12, final=True):
        self.nsteps = nsteps
        self.final = final

    def dram_in(self, name, shape, dt=F32):
        return self.nc.dram_tensor(name, list(shape), dt, kind="ExternalInput").ap()

    def build(self):
        nc = self.nc = bass.Bass("TRN2", target_bir_lowering=False)
        di = self.dram_in
        self.x = di("x", [S, D])
        self.gains = di("gains", [128, 96])
        self.fnorm = di("fnorm", [1, D])
        self.ffn_w1 = [di("ffn1_w1", [4, D, DFF]), di("ffn2_w1", [4, D, DFF])]
        self.ffn_w3 = [di("ffn1_w3", [4, D, DFF]), di("ffn2_w3", [4, D, DFF])]
        self.ffn_w2 = [di("ffn1_w2", [4, DFF, D]), di("ffn2_w2", [4, DFF, D])]
        self.even_w_in = di("even_w_in", [2, D, 3072])
        self.even_w_out = di("even_w_out", [2, D, D])
        self.odd_w_in = di("odd_w_in", [2, D, 1440])
        self.odd_w_out = di("odd_w_out", [2, D, D])
        self.w_qb = di("mla_w_qb", [2, 384, 768])
        self.w_kvb = di("mla_w_kvb", [2, 256, 1024])
        self.mla_g = di("mla_g", [128, 10])
        self.sink = di("sink", [1, 16])
        self.narpb = di("narpb", [2, 8, 2, 128, 896])
        self.namask = di("namask", [2, 128, 896])
        self.cst = di("cst", [6, 128, 128])
        self.rope64 = di("rope64", [2, 128, S])
        self.rope32 = di("rope32", [2, 128, S])
        self.out = nc.dram_tensor("out", [S, D], F32, kind="ExternalOutput").ap()
        self.vd = nc.dram_tensor("vd", [S, 8, 128], BF16, kind="ExternalOutput").ap()

        st = ExitStack()
        with st:
            t16 = st.enter_context(nc.sbuf_tensor("arena", [128, ARENA_KIB * 512], BF16))
            self.ar = Arena(t16)
            self.ps = [st.enter_context(nc.psum_tensor("ps%d" % i, [128, 512], F32)) for i in range(8)]
            self.P = Prog(nc)
            self.consts()
            step = 0
            src = self.x
            for i in range(4):
                for sub in range(3):
                    if step >= self.nsteps:
                        break
                    if sub == 0:
                        if not SKIPFFN:
                            self.ffn(i, 0, src)
                    elif sub == 1:
                        (self.even_mixer if i % 2 == 0 else self.odd_mixer)(i)
                    else:
                        self.ffn(i, 1, src)
                    src = self.out
                    step += 1
            if self.final:
                self.final_norm(src)
            self.P.emit(st)
        return nc

    def A(self, eng, fn, reads=(), writes=(), dma=False):
        return self.P.add(eng, fn, reads, writes, dma)

    def dma(self, out, in_, reads=(), writes=(), eng="sp"):
        return self.P.add(eng, lambda e: e.dma_start(out=out, in_=in_), reads, writes, dma=True)

    def mm(self, out, pairs, reads, writes):
        def fn(e):
            n = len(pairs)
            for i, (l, r) in enumerate(pairs):
                ins = e.matmul(out, lhsT=l, rhs=r, start=(i == 0), stop=(i == n - 1))
            return ins
        return self.P.add("pe", fn, reads, writes)

    def consts(self):
        ar = self.ar
        self.gt = ar.alloc(96, F32)
        self.mg = ar.alloc(16, F32)
        self.epsb = ar.alloc(8, F32)
        self.esink = ar.alloc(16, F32)
        self.swapf = ar.alloc(128, F32)
        self.ident = ar.alloc(128, BF16)
        self.R64 = ar.alloc(128, BF16)
        self.R32 = ar.alloc(128, BF16)
        self.M2 = ar.alloc(256, BF16)
        self.Mge = self.M2[:, 0:128]
        self.Mle = self.M2[:, 128:256]
        self.ones_col = ar.alloc(64, BF16)
        self.stat = ar.alloc(64, F32)
        self.dma(self.gt, self.gains, writes=["c_gt"])
        self.dma(self.mg[:, 0:10], self.mla_g, writes=["c_mg"])
        self.dma(self.swapf, self.cst[1], writes=["c_swap"])
        sk = bass.AP(self.sink.tensor, self.sink.offset, [[0, 128], [1, 16]])
        self.dma(self.esink, sk, writes=["c_sink"])
        for k, t in ((0, self.ident), (2, self.R64), (3, self.R32), (4, self.Mge), (5, self.Mle)):
            self.dma(t, self.cst[k], writes=["c_bf%d" % k], eng="pool")
        self.A("pool", lambda e: e.memset(self.epsb, 1e-6), writes=["c_eps"])
        self.A("pool", lambda e: e.memset(self.ones_col, 1.0), writes=["c_ones"])
        self.A("act", lambda e: e.activation(out=self.esink, in_=self.esink, func=AF.Exp),
               reads=["c_sink"], writes=["c_sink"])
        self.cmark = ar.off
        self.P.barrier()
        self.statk = 0

    def phase(self):
        self.P.barrier()
        self.ar.off = self.cmark

    def norm_tile(self, xt, xs, junk, rx, rxs, width=D, eps_scale=None):
        k = self.statk % 16
        self.statk += 1
        ss = self.stat[:, k:k + 1]
        rs = self.stat[:, 16 + k:17 + k]
        rr = self.stat[:, 32 + k:33 + k]
        inv = 1.0 / width
        self.A("dve", lambda e: e.scalar_tensor_tensor(out=junk, in0=xt, scalar=inv, in1=xt, op0=ALU.mult,
                                                         op1=ALU.mult, accum_out=ss),
               reads=[rx], writes=[("junk",), ("ss", k)])
        self.A("act", lambda e: e.activation(out=rs, in_=ss, func=AF.Sqrt, bias=self.epsb[:, 0:1], scale=1.0),
               reads=[("ss", k)], writes=[("rs", k)])
        self.A("dve", lambda e: e.reciprocal(out=rr, in_=rs), reads=[("rs", k)], writes=[("rr", k)])
        self.A("dve", lambda e: e.tensor_scalar(out=xs, in0=xt, scalar1=rr, scalar2=None, op0=ALU.mult),
               reads=[rx, ("rr", k)], writes=[rxs])
        return rr, ("rr", k)

    def alloc_norm_bufs(self, nxt=3, nh=1):
        ar = self.ar
        self.xt = [ar.alloc(D, F32) for _ in range(nxt)]
        self.xs = [ar.alloc(D, BF16) for _ in range(2)]
        self.junk = ar.alloc(D, BF16)
        self.hnTs = [ar.alloc(8 * 512, BF16).rearrange("p (c n) -> p c n", c=8) for _ in range(nh)]
        self.hnT = self.hnTs[0]
        self.ntile = 0

    def norm_chunk(self, src, c, g0, buf=0, tiles=(0, 1, 2, 3)):
        psT = self.ps[0].bitcast(BF16)
        for t in tiles:
            tile = 4 * c + t
            n = self.ntile
            self.ntile += 1
            xt = self.xt[n % len(self.xt)]
            xs = self.xs[n % 2]
            rx = ("xt", n % len(self.xt))
            rxs = ("xs", n % 2)
            self.dma(xt, src[tile * 128:(tile + 1) * 128, :], reads=[("H", tile)], writes=[rx])
            self.norm_tile(xt, xs, self.junk, rx, rxs)

            def tr(e, xs=xs):
                for cc in range(8):
                    ins = e.transpose(out=psT[:, cc * 128:(cc + 1) * 128], in_=xs[:, cc * 128:(cc + 1) * 128],
                                      identity=self.ident)
                return ins
            self.A("pe", tr, reads=[rxs], writes=[("ps", 0)])
            gb = bc_last(self.gt[:, g0:g0 + 8], 128)
            hdst = self.hnTs[buf][:, :, t * 128:(t + 1) * 128]
            self.A("dve", lambda e, hdst=hdst, gb=gb: e.tensor_tensor(
                out=hdst, in0=psT.rearrange("p (c n) -> p c n", c=8), in1=gb,
                op=ALU.mult), reads=[("ps", 0)], writes=[("hnT", buf)])

    def ffn(self, i, which, src):
        self.phase()
        ar = self.ar
        w1b = ar.alloc(8 * DFF, BF16).rearrange("p (c n) -> p c n", c=8)
        w3b = ar.alloc(8 * DFF, BF16).rearrange("p (c n) -> p c n", c=8)
        w2b = ar.alloc(22 * D, BF16).rearrange("p (c n) -> p c n", c=22)
        w1 = self.ffn_w1[which][i].rearrange("(c p) n -> p c n", p=128)
        w3 = self.ffn_w3[which][i].rearrange("(c p) n -> p c n", p=128)
        w2 = self.ffn_w2[which][i].rearrange("(c p) n -> p c n", p=128)
        for h in range(2):
            sl = slice(h * 1408, (h + 1) * 1408)
            self.dma(w1b[:, :, sl], w1[:, :, sl], writes=[("w1", h)], eng="pool")
            self.dma(w3b[:, :, sl], w3[:, :, sl], writes=[("w3", h)], eng="pool")
        for h in range(2):
            self.dma(w2b[:, h * 11:(h + 1) * 11, :], w2[:, h * 11:(h + 1) * 11, :], writes=[("w2", h)], eng="pool")
        self.alloc_norm_bufs(3)
        ht = [ar.alloc(D, F32) for _ in range(2)]
        gT = ar.alloc(22 * 512, BF16).rearrange("p (c n) -> p c n", c=22)
        slb = [ar.alloc(512, BF16) for _ in range(2)]
        g0 = (0 if which == 0 else 64) + 8 * i
        nres = 0
        self.norm_chunk(src, 0, g0)
        for c in range(8):
            for fc in range(22):
                b = fc % 2
                h = 0 if fc < 11 else 1
                p1, p3 = self.ps[1 + b], self.ps[3 + b]
                fs = slice(fc * 128, (fc + 1) * 128)
                self.mm(p1[:, :], [(w1b[:, dc, fs], self.hnT[:, dc, :]) for dc in range(8)],
                        reads=[("w1", h), ("hnT", 0)], writes=[("ps", 1 + b)])
                self.mm(p3[:, :], [(w3b[:, dc, fs], self.hnT[:, dc, :]) for dc in range(8)],
                        reads=[("w3", h), ("hnT", 0)], writes=[("ps", 3 + b)])
                self.A("act", lambda e, p1=p1, b=b: e.activation(out=slb[b], in_=p1[:, :], func=AF.Silu),
                       reads=[("ps", 1 + b)], writes=[("slb", b)])
                self.A("dve", lambda e, p3=p3, b=b, fc=fc: e.tensor_tensor(out=gT[:, fc, :], in0=slb[b], in1=p3[:, :],
                                                                           op=ALU.mult),
                       reads=[("ps", 3 + b), ("slb", b)], writes=[("gT", fc)])
            if c + 1 < 8:
                self.norm_chunk(src, c + 1, g0)
            for t in range(4):
                tile = 4 * c + t
                hb = nres % 2
                self.dma(ht[hb], src[tile * 128:(tile + 1) * 128, :], reads=[("H", tile)], writes=[("ht", hb)])
                for half in range(2):
                    pb = 5 + (2 * nres + half) % 3
                    po = self.ps[pb]
                    hs = slice(half * 512, (half + 1) * 512)
                    self.mm(po[:, :], [(gT[:, fc, t * 128:(t + 1) * 128], w2b[:, fc, hs]) for fc in range(22)],
                            reads=[("gT", fc) for fc in range(22)] + [("w2", 0), ("w2", 1)], writes=[("ps", pb)])
                    self.A("dve", lambda e, po=po, hb=hb, hs=hs: e.scalar_tensor_tensor(
                        out=ht[hb][:, hs], in0=po[:, :], scalar=0.5, in1=ht[hb][:, hs], op0=ALU.mult, op1=ALU.add),
                        reads=[("ps", pb), ("ht", hb)], writes=[("ht", hb)])
                self.dma(self.out[tile * 128:(tile + 1) * 128, :], ht[hb], reads=[("ht", hb)], writes=[("H", tile)])
                nres += 1

    def final_norm(self, src):
        self.phase()
        ar = self.ar
        gf = ar.alloc(D, F32)
        fb = bass.AP(self.fnorm.tensor, self.fnorm.offset, [[0, 128], [1, D]])
        self.dma(gf, fb, writes=["gf"])
        xt = [ar.alloc(D, F32) for _ in range(3)]
        junk = ar.alloc(D, BF16)
        for tile in range(NT):
            b = tile % 3
            k = self.statk % 16
            self.statk += 1
            ss = self.stat[:, k:k + 1]
            rs = self.stat[:, 16 + k:17 + k]
            rr = self.stat[:, 32 + k:33 + k]
            self.dma(xt[b], src[tile * 128:(tile + 1) * 128, :], reads=[("H", tile)], writes=[("xt", b)])
            self.A("dve", lambda e, b=b, ss=ss: e.scalar_tensor_tensor(out=junk, in0=xt[b], scalar=1.0 / D, in1=xt[b],
                                                                       op0=ALU.mult, op1=ALU.mult, accum_out=ss),
                   reads=[("xt", b)], writes=[("junk",), ("ss", k)])
            self.A("act", lambda e, ss=ss, rs=rs: e.activation(out=rs, in_=ss, func=AF.Sqrt, bias=self.epsb[:, 0:1],
                                                               scale=1.0), reads=[("ss", k)], writes=[("rs", k)])
            self.A("dve", lambda e, rs=rs, rr=rr: e.reciprocal(out=rr, in_=rs), reads=[("rs", k)], writes=[("rr", k)])
            self.A("dve", lambda e, b=b, rr=rr: e.scalar_tensor_tensor(out=xt[b], in0=xt[b], scalar=rr, in1=gf,
                                                                       op0=ALU.mult, op1=ALU.mult),
                   reads=[("xt", b), ("rr", k), "gf"], writes=[("xt", b)])
            self.dma(self.out[tile * 128:(tile + 1) * 128, :], xt[b], reads=[("xt", b)], writes=[("H", tile)])

    def rope_store(self, psq, rows, dst, cosb, sinb, Rm, qb, t1, t2, rd_cs, wr_dst, pbank, rbank):
        pr = self.ps[rbank]
        k = self.nrope % 2
        self.nrope += 1
        qbk = self.qbs[k]
        qk = ("qb", k)
        self.A("act", lambda e: e.activation(out=qbk[rows, :], in_=psq[rows, :], func=AF.Copy),
               reads=[("ps", pbank)], writes=[qk])

        def part_b():
            self.mm(pr[rows, :], [(Rm[rows, rows], qbk[rows, :])], reads=[qk], writes=[("ps", rbank)])
            self.A("dve", lambda e: e.tensor_tensor(out=t1[rows, :], in0=psq[rows, :], in1=cosb[rows, :],
                                                    op=ALU.mult),
                   reads=[("ps", pbank)] + rd_cs, writes=[("t1",)])
            self.A("dve", lambda e: e.tensor_tensor(out=t2[rows, :], in0=pr[rows, :], in1=sinb[rows, :],
                                                    op=ALU.mult),
                   reads=[("ps", rbank)] + rd_cs, writes=[("t2",)])
            self.A("dve", lambda e: e.tensor_tensor(out=dst, in0=t1[rows, :], in1=t2[rows, :], op=ALU.add),
                   reads=[("t1",), ("t2",)], writes=wr_dst)
        prev = self.rope_pending
        self.rope_pending = part_b
        return prev

    def rope_flush(self):
        if self.rope_pending is not None:
            f = self.rope_pending
            self.rope_pending = None
            f()

    def load_v(self, vb, head, dil, rkey):
        per = NT // dil
        vdt = self.vd.tensor
        for r in range(dil):
            cps = max(1, per // 4) if dil == 1 else per
            for c0 in range(0, per, cps):
                off = self.vd.offset + ((r + dil * 128 * c0) * 8 + head) * 128
                src = bass.AP(vdt, off, [[dil * 1024, 128], [dil * 128 * 1024, cps], [1, 128]])
                self.dma(vb[:, r * per + c0:r * per + c0 + cps, :], src, reads=[("VD",)], writes=[rkey])

    def finalize(self, po, pbank, rows, dst, wr_dst, oa, k, sink_fn=None):
        n = dst.shape[-1] if False else None
        oak = ("oa", k)
        self.A("act", lambda e: e.activation(out=oa, in_=po, func=AF.Copy), reads=[("ps", pbank)], writes=[oak])
        if sink_fn is not None:
            sink_fn(oa, oak)
        self.A("dve", lambda e: e.reciprocal(out=oa[64:128, :], in_=oa[64:128, :]), reads=[oak], writes=[oak])
        sb = 6 + k
        psw = self.ps[sb]
        self.mm(psw[:, 0:oa.shape[1]], [(self.swapf, oa)], reads=[oak, "c_swap"], writes=[("ps", sb)])
        i0 = oa[rows, :]
        i1 = psw[rows, 0:oa.shape[1]]
        if len(dst.shape) == 3:
            i0 = i0.rearrange("p (j n) -> p j n", j=dst.shape[1])
            i1 = i1.rearrange("p (j n) -> p j n", j=dst.shape[1])
        self.A("dve", lambda e: e.tensor_tensor(out=dst, in0=i0, in1=i1, op=ALU.mult),
               reads=[oak, ("ps", sb)], writes=wr_dst)

    def fin1(self, po, pbank, oa, oak, sink_fn=None, eng="act"):
        if eng == "act":
            self.A("act", lambda e: e.activation(out=oa, in_=po, func=AF.Copy), reads=[("ps", pbank)], writes=[oak])
        else:
            self.A("dve", lambda e: e.tensor_copy(out=oa, in_=po), reads=[("ps", pbank)], writes=[oak])
        if sink_fn is not None:
            sink_fn(oa, oak)
        if eng == "act":
            self.A("act", lambda e: e.activation(out=oa[64:128, :], in_=oa[64:128, :], func=AF.Ln), reads=[oak],
                   writes=[oak])
            self.A("act", lambda e: e.activation(out=oa[64:128, :], in_=oa[64:128, :], func=AF.Exp, scale=-1.0),
                   reads=[oak], writes=[oak])
        else:
            self.A("dve", lambda e: e.reciprocal(out=oa[64:128, :], in_=oa[64:128, :]), reads=[oak], writes=[oak])

    def fin2(self, rows, dst, wr_dst, oa, oak, sb):
        psw = self.ps[sb]
        n = oa.shape[1]
        self.mm(psw[:, 0:n], [(self.swapf, oa)], reads=[oak, "c_swap"], writes=[("ps", sb)])
        i0 = oa[rows, :]
        i1 = psw[rows, 0:n]
        if len(dst.shape) == 3:
            i0 = i0.rearrange("p (j n) -> p j n", j=dst.shape[1])
            i1 = i1.rearrange("p (j n) -> p j n", j=dst.shape[1])
        self.A("dve", lambda e: e.tensor_tensor(out=dst, in0=i0, in1=i1, op=ALU.mult),
               reads=[oak, ("ps", sb)], writes=wr_dst)

    def run_pipe(self, items, L, D, G=1):
        pend = []
        n = len(items)
        total = n + L
        i = 0
        while i < total:
            hi = min(i + G, total)
            for a in range(i, hi):
                if a < n:
                    it = items[a]
                    if "pre" in it:
                        it["pre"]()
                    it["ab"]()
            for a in range(i, hi):
                jx = a - L
                if 0 <= jx < n:
                    it = items[jx]
                    it["c"]()
                    if "f1" in it:
                        it["f1"]()
                        pend.append((jx + D, it["f2"]))
                    while pend and pend[0][0] <= jx:
                        pend.pop(0)[1]()
            i = hi
        for _, f in pend:
            f()

    def out_proj(self, mixedT, w_rows, src):
        ar = self.ar
        wo = ar.alloc(8 * D, BF16).rearrange("p (c n) -> p c n", c=8)
        for c, parts in enumerate(w_rows):
            for (rows_ap, psl) in parts:
                self.dma(wo[psl, c, :], rows_ap, writes=[("wo", c, psl.start)], eng="pool")
        wkeys = [("wo", c, psl.start) for c, parts in enumerate(w_rows) for (_, psl) in parts]
        ht = [ar.alloc(D, F32) for _ in range(4)]
        for tile in range(NT):
            hb = tile % 4
            self.dma(ht[hb], src[tile * 128:(tile + 1) * 128, :], reads=[("H", tile)], writes=[("ht", hb)])
            for half in range(2):
                pb = 1 + (2 * tile + half) % 6
                po = self.ps[pb]
                hs = slice(half * 512, (half + 1) * 512)
                self.mm(po[:, :], [(mixedT[:, fc, tile * 128:(tile + 1) * 128], wo[:, fc, hs]) for fc in range(8)],
                        reads=[("mixedT",)] + wkeys, writes=[("ps", pb)])
                self.A("dve", lambda e, po=po, hb=hb, hs=hs: e.tensor_tensor(out=ht[hb][:, hs], in0=ht[hb][:, hs],
                                                                              in1=po[:, :], op=ALU.add),
                       reads=[("ps", pb), ("ht", hb)], writes=[("ht", hb)])
            self.dma(self.out[tile * 128:(tile + 1) * 128, :], ht[hb], reads=[("ht", hb)], writes=[("H", tile)])

    def alloc_proj_tmps(self, rope64=True):
        ar = self.ar
        if rope64:
            self.cosb = [ar.alloc(512, F32) for _ in range(2)]
            self.sinb = [ar.alloc(512, F32) for _ in range(2)]
        self.qb = ar.alloc(512, BF16)
        self.qbs = [self.qb, ar.alloc(512, BF16)]
        self.nrope = 0
        self.rope_pending = None
        self.t1 = ar.alloc(512, F32)
        self.t2 = ar.alloc(512, F32)
        self.vtok = [ar.alloc(8 * 128, BF16).rearrange("p (h d) -> p h d", h=8) for _ in range(2)]
        for b in range(2):
            vo = self.vtok[b][:, :, 64:128]
            self.A("pool", lambda e, vo=vo: e.memset(vo, 1.0), writes=[("vtok", b)])
        self.nv = 0

    def v_store(self, psv, pbank, nheads, tile):
        b = self.nv % 2
        self.nv += 1
        vt = self.vtok[b]
        self.A("act", lambda e: e.activation(out=vt[:, 0:nheads, 0:64], in_=psv, func=AF.Copy),
               reads=[("ps", pbank)], writes=[("vtok", b)])
        self.dma(self.vd[tile * 128:(tile + 1) * 128, 0:nheads, :], vt[:, 0:nheads, :], reads=[("vtok", b)],
                 writes=[("VD",)])

    def qkv_proj_even(self, i, j, col0, rope, qT, kT, src):
        ar = self.ar
        mark = ar.off
        wq = ar.alloc(8 * 512, BF16).rearrange("p (c n) -> p c n", c=8)
        wk = ar.alloc(8 * 512, BF16).rearrange("p (c n) -> p c n", c=8)
        wv = ar.alloc(8 * 512, BF16).rearrange("p (c n) -> p c n", c=8)
        win = self.even_w_in[j].rearrange("(c p) n -> p c n", p=128)
        for k, w in enumerate((wq, wk, wv)):
            self.dma(w, win[:, :, col0 + k * 512:col0 + (k + 1) * 512], writes=[("wqkv", k)], eng="pool")
        self.alloc_norm_bufs(2, 2)
        self.alloc_proj_tmps()
        g0 = 32 + 8 * i
        npj = 0
        self.norm_chunk(src, 0, g0, 0)
        for c in range(8):
            cs = slice(c * 512, (c + 1) * 512)
            hn = self.hnTs[c % 2]
            hk = ("hnT", c % 2)
            cb = c % 2
            if rope:
                self.dma(self.cosb[cb], self.rope64[0][:, cs], writes=[("cos", cb)])
                self.dma(self.sinb[cb], self.rope64[1][:, cs], writes=[("sin", cb)])
            for hp in range(4):
                for k, (w, dstT) in enumerate(((wq, qT), (wk, kT))):
                    pb = 1 + npj % 3
                    rb = 4 + npj % 2
                    npj += 1
                    pq = self.ps[pb]
                    self.mm(pq[:, :], [(w[:, dc, hp * 128:(hp + 1) * 128], hn[:, dc, :]) for dc in range(8)],
                            reads=[("wqkv", k), hk], writes=[("ps", pb)])
                    dst = dstT[:, hp, cs]
                    wr = [("qkT", k, hp)]
                    if rope and DBG != 11:
                        prev = self.rope_store(pq, slice(0, 128), dst, self.cosb[cb], self.sinb[cb], self.R64,
                                               self.qb, self.t1, self.t2, [("cos", cb), ("sin", cb)], wr, pb, rb)
                        if prev is not None:
                            prev()
                    else:
                        self.A("act", lambda e, pq=pq, dst=dst: e.activation(out=dst, in_=pq[:, :], func=AF.Copy),
                               reads=[("ps", pb)], writes=wr)
                if c + 1 < 8 and hp in (0, 2):
                    self.norm_chunk(src, c + 1, g0, (c + 1) % 2, tiles=(hp, hp + 1))
            if rope:
                self.rope_flush()
            for t in range(4):
                pb = 6 + t % 2
                pv = self.ps[pb]
                self.mm(pv[:, :], [(hn[:, dc, t * 128:(t + 1) * 128], wv[:, dc, :]) for dc in range(8)],
                        reads=[("wqkv", 2), hk], writes=[("ps", pb)])
                if DBG != 12:
                    self.v_store(pv[:, :].rearrange("p (h d) -> p h d", h=8), pb, 8, 4 * c + t)
        self.P.barrier()
        ar.off = mark

    def even_mixer(self, i):
        j = i // 2
        self.phase()
        ar = self.ar
        src = self.x if (SKIPFFN and i == 0) else self.out
        mixedT = ar.alloc(8 * S, BF16).rearrange("p (c n) -> p c n", c=8)
        qT = ar.alloc(4 * S, BF16).rearrange("p (c n) -> p c n", c=4)
        kT = ar.alloc(4 * S, BF16).rearrange("p (c n) -> p c n", c=4)
        self.qkv_proj_even(i, j, 0, True, qT, kT, src)
        if DBG in (1, 11, 12):
            return
        mark = ar.off
        vb = [ar.alloc(32 * 128, BF16).rearrange("p (c n) -> p c n", c=32) for _ in range(2)]
        accs = [ar.alloc(S, F32) for _ in range(2)]
        pt = [ar.alloc(512, BF16) for _ in range(8)]
        M3p = ar.alloc(512, BF16)
        for kk in range(2):
            self.A("pool", lambda e, kk=kk: e.tensor_copy(out=M3p[:, 256 * kk:256 * kk + 128], in_=self.Mle),
                   writes=[("M3",)])
            self.A("pool", lambda e, kk=kk: e.tensor_copy(out=M3p[:, 256 * kk + 128:256 * kk + 256], in_=self.Mge),
                   writes=[("M3",)])
        nit = 0
        nvb = 0
        qkr = lambda hp: [("qkT", 0, hp), ("qkT", 1, hp)]
        for h in range(8):
            hp, base = h // 2, (h % 2) * 64
            rows = slice(base, base + 64)
            items = []
            pres = []
            acc = accs[h % 2]
            ak = ("acc", h % 2)
            self.A("pool", lambda e, acc=acc: e.memset(acc, 0.0),
                   writes=[ak] + [("accf", h % 2, c) for c in range(8)])
            for bi, dil in enumerate((1, 4, 16)):
                vbb = vb[nvb % 2]
                vkey = ("vb", nvb % 2)
                nvb += 1
                pres.append((len(items), (lambda vbb=vbb, h=h, dil=dil, vkey=vkey: self.load_v(vbb, h, dil, vkey))))
                per = NT // dil
                sd = S // dil
                if per % 4 == 0:
                    groups = [(b, b + 2) for b4 in range(0, per, 4) for b in (b4, b4 + 1)]
                else:
                    groups = [(b,) for b in range(per)]
                for r in range(dil):
                    for grp_ in groups:
                        segs = []
                        col = 0
                        for b in grp_:
                            q0 = max(0, 128 * b - 64)
                            q1 = min(sd, 128 * b + 192)
                            segs.append((b, q0, q1 - q0, col))
                            col += q1 - q0
                        W = col
                        ms = 64 if grp_[0] == 0 else 0
                        if len(segs) == 2:
                            assert segs[0][1] + segs[0][2] == segs[1][1] and ms + segs[0][2] == 256
                        sbk = nit % 4
                        pk = nit % 8
                        ob = 4 + nit % 4
                        nit += 1
                        pS = self.ps[sbk]
                        po = self.ps[ob]
                        mms = [(strided(kT[rows, hp, :], (128 * b) * dil + r, dil, 128),
                                strided(qT[rows, hp, :], q0 * dil + r, dil, w), col_, w, vbb[:, r * per + b, :])
                               for (b, q0, w, col_) in segs]
                        msk = M3p[:, ms:ms + W]
                        ptk = pt[pk]
                        dsta = strided(acc, segs[0][1] * dil + r, dil, W)

                        def ab(pS=pS, mms=mms, W=W, sbk=sbk, pk=pk, msk=msk, ptk=ptk, hp=hp):
                            def smm(e):
                                for (kap, qap, col_, w, lv) in mms:
                                    ins = e.matmul(pS[:, col_:col_ + w], lhsT=kap, rhs=qap, start=True, stop=True)
                                return ins
                            self.A("pe", smm, reads=qkr(hp), writes=[("ps", sbk)])
                            self.A("act", lambda e: e.activation(out=ptk[:, 0:W], in_=pS[:, 0:W], func=AF.Exp,
                                                                 scale=0.125),
                                   reads=[("ps", sbk)], writes=[("pt", pk)])
                            self.A(MASK_ENG, lambda e: e.tensor_tensor(out=ptk[:, 0:W], in0=ptk[:, 0:W], in1=msk,
                                                                       op=ALU.mult),
                                   reads=[("pt", pk), ("M3",)], writes=[("pt", pk)])

                        def cc(po=po, mms=mms, W=W, ob=ob, pk=pk, ptk=ptk, vkey=vkey, dsta=dsta, ak=ak):
                            def pvm(e):
                                for (kap, qap, col_, w, lv) in mms:
                                    ins = e.matmul(po[:, col_:col_ + w], lhsT=lv, rhs=ptk[:, col_:col_ + w], start=True,
                                                   stop=True)
                                return ins
                            self.A("pe", pvm, reads=[vkey, ("pt", pk)], writes=[("ps", ob)])
                            self.A("dve", lambda e: e.tensor_tensor(out=dsta, in0=dsta, in1=po[:, 0:W], op=ALU.add),
                                   reads=[("ps", ob), ak], writes=[ak])
                        items.append(dict(ab=ab, c=cc))
            items[0]["pre"] = pres[0][1]
            items[pres[0][0] + 6]["pre"] = pres[1][1]
            items[pres[1][0] + 6]["pre"] = pres[2][1]
            self.run_pipe(items, 4, 0, 4)
            for c in range(8):
                cs = slice(c * 512, (c + 1) * 512)
                k = c % 2
                a = acc[:, cs]
                fk = ("accf", h % 2, c)
                self.A("act", lambda e, a=a: e.activation(out=a[64:128, :], in_=a[64:128, :], func=AF.Ln),
                       reads=[ak], writes=[fk])
                self.A("act", lambda e, a=a: e.activation(out=a[64:128, :], in_=a[64:128, :], func=AF.Exp, scale=-1.0),
                       reads=[fk], writes=[fk])
                psw = self.ps[6 + k]
                self.mm(psw[:, :], [(self.swapf, a)], reads=[fk, "c_swap"], writes=[("ps", 6 + k)])
                self.A("dve", lambda e, a=a, psw=psw, cs=cs, rows=rows, hp=hp: e.tensor_tensor(
                    out=mixedT[rows, hp, cs], in0=a[rows, :], in1=psw[rows, :], op=ALU.mult),
                    reads=[fk, ("ps", 6 + k)], writes=[("mixedT",)])
        self.P.barrier()
        ar.off = mark
        if DBG == 2:
            return
        self.qkv_proj_even(i, j, 1536, False, qT, kT, src)
        mark = ar.off
        vb = [ar.alloc(32 * 128, BF16).rearrange("p (c n) -> p c n", c=32) for _ in range(2)]
        Mi = ar.alloc(8 * 896, BF16).rearrange("p (h n) -> p h n", h=8)
        Mf = ar.alloc(8 * 896, BF16).rearrange("p (h n) -> p h n", h=8)
        raw = [ar.alloc(896, F32) for _ in range(2)]
        mk = [ar.alloc(896, F32) for _ in range(2)]
        pt = [ar.alloc(256, BF16) for _ in range(3)]
        oa = [ar.alloc(256, F32) for _ in range(2)]
        prev_start = 0
        for kind in range(2):
            self.dma(mk[kind], self.namask[kind], writes=[("mk", kind)])
        nr = 0
        for h in range(8):
            for kind, M in ((0, Mi), (1, Mf)):
                b = nr % 2
                nr += 1
                self.dma(raw[b], self.narpb[j, h, kind], writes=[("raw", b)])
                self.A("act", lambda e, b=b: e.activation(out=raw[b], in_=raw[b], func=AF.Exp), reads=[("raw", b)],
                       writes=[("raw", b)])
                self.A("dve", lambda e, b=b, M=M, h=h, kind=kind: e.tensor_tensor(out=M[:, h, :], in0=raw[b],
                                                                                  in1=mk[kind], op=ALU.mult),
                       reads=[("raw", b), ("mk", kind)], writes=[("M", kind, h)])
        pt += [ar.alloc(256, BF16) for _ in range(5)]
        items = []
        nit = 0
        grp = 0
        for h in range(8):
            hp, base = h // 2, (h % 2) * 64
            rows = slice(base, base + 64)
            vbb = vb[h % 2]
            vkey = ("vb", h % 2)
            pre = (lambda vbb=vbb, h=h, vkey=vkey: self.load_v(vbb, h, 1, vkey))
            hstart = len(items)
            for a in range(16):
                kind = 1 if a in (0, 15) else 0
                M = Mf if kind else Mi
                cl = list(range(max(0, 2 * a - 2), min(31, 2 * a + 3) + 1))
                ob = 4 + grp % 2
                k = grp % 2
                grp += 1
                po = self.ps[ob]
                qap = qT[rows, hp, 256 * a:256 * a + 256]
                for ci, cidx in enumerate(cl):
                    sbk = nit % 4
                    pk = nit % 8
                    nit += 1
                    pS = self.ps[sbk]
                    off = 4 * a - 2 * cidx
                    msl = M[:, h, (off + 6) * 64:(off + 10) * 64]
                    kap = kT[rows, hp, 128 * cidx:128 * cidx + 128]
                    lv = vbb[:, cidx, :]
                    ptk = pt[pk]

                    def ab(pS=pS, kap=kap, qap=qap, sbk=sbk, pk=pk, ptk=ptk, msl=msl, hp=hp, kind=kind, h=h):
                        self.mm(pS[:, 0:256], [(kap, qap)], reads=[("qkT", 0, hp), ("qkT", 1, hp)],
                                writes=[("ps", sbk)])
                        self.A("act", lambda e: e.activation(out=ptk, in_=pS[:, 0:256], func=AF.Exp, scale=0.125),
                               reads=[("ps", sbk)], writes=[("pt", pk)])
                        self.A("dve", lambda e: e.tensor_tensor(out=ptk, in0=ptk, in1=msl, op=ALU.mult),
                               reads=[("pt", pk), ("M", kind, h)], writes=[("pt", pk)])

                    def cc(po=po, lv=lv, ptk=ptk, pk=pk, ci=ci, nch=len(cl), vkey=vkey, ob=ob):
                        self.A("pe", lambda e: e.matmul(po[:, 0:256], lhsT=lv, rhs=ptk, start=(ci == 0),
                                                        stop=(ci == nch - 1)),
                               reads=[vkey, ("pt", pk)], writes=[("ps", ob)])
                    it = dict(ab=ab, c=cc)
                    if ci == len(cl) - 1:
                        dst = mixedT[rows, 4 + hp, 256 * a:256 * a + 256]
                        it["f1"] = (lambda po=po, ob=ob, k=k: self.fin1(po[:, 0:256], ob, oa[k], ("oa", k)))
                        it["f2"] = (lambda rows=rows, dst=dst, k=k: self.fin2(rows, dst, [("mixedT",)], oa[k],
                                                                              ("oa", k), 6 + k))
                    items.append(it)
            if h == 0:
                items[0]["pre"] = pre
            else:
                items[prev_start + 12]["pre"] = pre
            prev_start = hstart
        self.run_pipe(items, 4, 4, 4)
        self.P.barrier()
        ar.off = mark
        if DBG == 3:
            return
        wo = self.even_w_out[j]
        w_rows = [[(wo[c * 128:(c + 1) * 128, :], slice(0, 128))] for c in range(8)]
        self.out_proj(mixedT, w_rows, src)

    def odd_mixer(self, i):
        j = i // 2
        self.phase()
        ar = self.ar
        src = self.out
        qaT = ar.alloc(3 * S, BF16).rearrange("p (c n) -> p c n", c=3)
        kvaT = ar.alloc(2 * S, BF16).rearrange("p (c n) -> p c n", c=2)
        kTh = [ar.alloc(S, BF16) for _ in range(2)]
        mark_mla = ar.off
        qT = ar.alloc(4 * S, BF16).rearrange("p (c n) -> p c n", c=4)
        kT = ar.alloc(S, BF16)
        mark = ar.off
        wb = ar.alloc(8 * 1440, BF16).rearrange("p (c n) -> p c n", c=8)
        win = self.odd_w_in[j].rearrange("(c p) n -> p c n", p=128)
        for two in range(2):
            for jq in range(4):
                self.dma(wb[:, :, jq * 128 + two * 64:jq * 128 + two * 64 + 64],
                         win[:, :, two * 256 + jq * 64:two * 256 + jq * 64 + 64], writes=[("wb", 2 + two, jq)],
                         eng="pool")
        self.dma(wb[:, :, 512:768], win[:, :, 512:768], writes=[("wb", 0)], eng="pool")
        self.dma(wb[:, :, 768:1440], win[:, :, 768:1440], writes=[("wb", 1)], eng="pool")
        wbk = [("wb", 0), ("wb", 1)] + [("wb", 2 + two, jq) for two in range(2) for jq in range(4)]
        self.alloc_norm_bufs(2, 2)
        self.alloc_proj_tmps()
        c32 = [ar.alloc(512, F32) for _ in range(2)]
        s32 = [ar.alloc(512, F32) for _ in range(2)]
        qas = [ar.alloc(640, BF16) for _ in range(2)]
        g0 = 32 + 8 * i
        npj = 0
        nq = 0
        psT = self.ps[0].bitcast(BF16)
        self.norm_chunk(src, 0, g0, 0)
        for c in range(8):
            cs = slice(c * 512, (c + 1) * 512)
            hn = self.hnTs[c % 2]
            hk = ("hnT", c % 2)
            cb = c % 2
            self.dma(self.cosb[cb], self.rope64[0][:, cs], writes=[("cos", cb)])
            self.dma(self.sinb[cb], self.rope64[1][:, cs], writes=[("sin", cb)])
            self.dma(c32[cb][64:96, :], self.rope32[0][64:96, cs], writes=[("c32", cb)])
            self.dma(s32[cb][64:96, :], self.rope32[1][64:96, cs], writes=[("s32", cb)])
            for jq in range(5):
                pb = 1 + npj % 3
                rb = 4 + npj % 2
                npj += 1
                pq = self.ps[pb]
                if jq < 4:
                    def wsl(dc, jq=jq):
                        return wb[:, dc, 128 * jq:128 * jq + 128]
                    dst = qT[:, jq, cs]
                    wr = [("qT", jq)]
                else:
                    def wsl(dc):
                        return wb[:, dc, 512:640]
                    dst = kT[:, cs]
                    wr = [("kT",)]
                self.mm(pq[:, :], [(wsl(dc), hn[:, dc, :]) for dc in range(8)], reads=wbk + [hk],
                        writes=[("ps", pb)])
                prev = self.rope_store(pq, slice(0, 128), dst, self.cosb[cb], self.sinb[cb], self.R64, self.qb,
                                       self.t1, self.t2, [("cos", cb), ("sin", cb)], wr, pb, rb)
                if prev is not None:
                    prev()
                if c + 1 < 8 and jq in (1, 3):
                    self.norm_chunk(src, c + 1, g0, (c + 1) % 2, tiles=(jq - 1, jq))
            pb = 1 + npj % 3
            rb = 4 + npj % 2
            npj += 1
            pq = self.ps[pb]
            r32 = slice(64, 96)
            self.mm(pq[r32, :], [(wb[:, dc, 1408:1440], hn[:, dc, :]) for dc in range(8)],
                    reads=wbk + [hk], writes=[("ps", pb)])
            prev = self.rope_store(pq, r32, kTh[0][r32, cs], c32[cb], s32[cb], self.R32, self.qb, self.t1, self.t2,
                                   [("c32", cb), ("s32", cb)], [("kTh", 0)], pb, rb)
            if prev is not None:
                prev()
            self.rope_flush()
            self.A("pool", lambda e, cs=cs: e.tensor_copy(out=kTh[1][r32, cs], in_=kTh[0][r32, cs]),
                   reads=[("kTh", 0)], writes=[("kTh", 1)])
            for t in range(4):
                tile = 4 * c + t
                ts = slice(t * 128, (t + 1) * 128)
                pb = 6 + t % 2
                pv = self.ps[pb]
                self.mm(pv[:, 0:128], [(hn[:, dc, ts], wb[:, dc, 640:768]) for dc in range(8)],
                        reads=wbk + [hk], writes=[("ps", pb)])
                self.v_store(pv[:, 0:128].rearrange("p (h d) -> p h d", h=2), pb, 2, tile)
                pb = 1 + npj % 3
                npj += 1
                pa = self.ps[pb]
                self.mm(pa[:, :], [(hn[:, dc, ts], wb[:, dc, 768:1280]) for dc in range(8)],
                        reads=wbk + [hk], writes=[("ps", pb)])
                pb2 = 1 + npj % 3
                npj += 1
                pa2 = self.ps[pb2]
                self.mm(pa2[:, 0:128], [(hn[:, dc, ts], wb[:, dc, 1280:1408]) for dc in range(8)],
                        reads=wbk + [hk], writes=[("ps", pb2)])
                qs = qas[nq % 2]
                qk = ("qas", nq % 2)
                nq += 1
                k = self.statk % 16
                self.statk += 1
                ss, rs, rr = self.stat[:, k:k + 1], self.stat[:, 16 + k:17 + k], self.stat[:, 32 + k:33 + k]
                k2 = self.statk % 16
                self.statk += 1
                ss2, ss3, rs2, rr2 = (self.stat[:, k2:k2 + 1], self.stat[:, 48 + k2 % 8:49 + k2 % 8],
                                      self.stat[:, 16 + k2:17 + k2], self.stat[:, 32 + k2:33 + k2])
                jk = self.junk
                self.A("act", lambda e, pa=pa, ss=ss: e.activation(out=jk[:, 0:384], in_=pa[:, 0:384], func=AF.Square,
                                                                   scale=float(384 ** -0.5), accum_out=ss),
                       reads=[("ps", pb)], writes=[("junk",), ("ss", k)])
                self.A("act", lambda e, ss=ss, rs=rs: e.activation(out=rs, in_=ss, func=AF.Sqrt,
                                                                   bias=self.epsb[:, 0:1], scale=1.0),
                       reads=[("ss", k)], writes=[("rs", k)])
                self.A("dve", lambda e, rs=rs, rr=rr: e.reciprocal(out=rr, in_=rs), reads=[("rs", k)],
                       writes=[("rr", k)])
                self.A("dve", lambda e, pa=pa, qs=qs, rr=rr: e.tensor_scalar(out=qs[:, 0:384], in0=pa[:, 0:384],
                                                                             scalar1=rr, scalar2=None, op0=ALU.mult),
                       reads=[("ps", pb), ("rr", k)], writes=[qk])
                self.A("act", lambda e, pa=pa, ss2=ss2: e.activation(out=jk[:, 384:512], in_=pa[:, 384:512],
                                                                     func=AF.Square, scale=1.0 / 16, accum_out=ss2),
                       reads=[("ps", pb)], writes=[("junk",), ("ss", k2)])
                self.A("act", lambda e, pa2=pa2, ss3=ss3: e.activation(out=jk[:, 512:640], in_=pa2[:, 0:128],
                                                                       func=AF.Square, scale=1.0 / 16,
                                                                       accum_out=ss3),
                       reads=[("ps", pb2)], writes=[("junk",), ("ss3", k2 % 8)])
                self.A("dve", lambda e, ss2=ss2, ss3=ss3: e.tensor_tensor(out=ss2, in0=ss2, in1=ss3, op=ALU.add),
                       reads=[("ss", k2), ("ss3", k2 % 8)], writes=[("ss", k2)])
                self.A("act", lambda e, ss2=ss2, rs2=rs2: e.activation(out=rs2, in_=ss2, func=AF.Sqrt,
                                                                       bias=self.epsb[:, 0:1], scale=1.0),
                       reads=[("ss", k2)], writes=[("rs", k2)])
                self.A("dve", lambda e, rs2=rs2, rr2=rr2: e.reciprocal(out=rr2, in_=rs2), reads=[("rs", k2)],
                       writes=[("rr", k2)])
                self.A("dve", lambda e, pa=pa, qs=qs, rr2=rr2: e.tensor_scalar(
                    out=qs[:, 384:512], in0=pa[:, 384:512], scalar1=rr2, scalar2=None, op0=ALU.mult),
                    reads=[("ps", pb), ("rr", k2)], writes=[qk])
                self.A("dve", lambda e, pa2=pa2, qs=qs, rr2=rr2: e.tensor_scalar(
                    out=qs[:, 512:640], in0=pa2[:, 0:128], scalar1=rr2, scalar2=None, op0=ALU.mult),
                    reads=[("ps", pb2), ("rr", k2)], writes=[qk])

                def tr(e, qs=qs):
                    for cc in range(5):
                        ins = e.transpose(out=psT[:, cc * 128:(cc + 1) * 128], in_=qs[:, cc * 128:(cc + 1) * 128],
                                          identity=self.ident)
                    return ins
                self.A("pe", tr, reads=[qk], writes=[("ps", 0)])
                gq = bc_last(self.mg[:, 5 * j:5 * j + 3], 128)
                gk = bc_last(self.mg[:, 5 * j + 3:5 * j + 5], 128)
                cols = slice(tile * 128, (tile + 1) * 128)
                self.A("dve", lambda e, gq=gq, cols=cols: e.tensor_tensor(
                    out=qaT[:, :, cols], in0=psT[:, 0:384].rearrange("p (c n) -> p c n", c=3), in1=gq, op=ALU.mult),
                    reads=[("ps", 0), "c_mg"], writes=[("qaT",)])
                self.A("dve", lambda e, gk=gk, cols=cols: e.tensor_tensor(
                    out=kvaT[:, :, cols], in0=psT[:, 384:640].rearrange("p (c n) -> p c n", c=2), in1=gk,
                    op=ALU.mult), reads=[("ps", 0), "c_mg"], writes=[("kvaT",)])
        self.P.barrier()
        ar.off = mark
        mixedT = ar.alloc(8 * S, BF16).rearrange("p (c n) -> p c n", c=8)
        mark2 = ar.off
        vb = [ar.alloc(32 * 128, BF16).rearrange("p (c n) -> p c n", c=32) for _ in range(2)]
        pt = [ar.alloc(512, BF16) for _ in range(8)]
        oa = [ar.alloc(512, F32) for _ in range(4)]
        items = []
        nit = 0
        grp = 0
        qrd = [("qT", q) for q in range(4)] + [("kT",)]
        for g in range(2):
            rows = slice(64 * g, 64 * g + 64)
            vbb = vb[g]
            vkey = ("vb", g)
            self.load_v(vbb, g, 1, vkey)
            for qb_ in range(NT):
                cl = [c for c in (qb_ - 1, qb_, qb_ + 1) if 0 <= c < NT]
                ob = 4 + grp % 2
                k = grp % 4
                sb = 6 + grp % 2
                grp += 1
                po = self.ps[ob]
                qap = qT[rows, :, 128 * qb_:128 * qb_ + 128]
                for ci, cidx in enumerate(cl):
                    sbk = nit % 4
                    pk = nit % 8
                    nit += 1
                    pS = self.ps[sbk]
                    ptk = pt[pk]
                    kap = kT[rows, 128 * cidx:128 * cidx + 128]
                    msk = None if cidx == qb_ else (self.Mge if cidx < qb_ else self.Mle)
                    lv = vbb[:, cidx, :]

                    def ab(pS=pS, kap=kap, qap=qap, sbk=sbk, pk=pk, ptk=ptk, msk=msk):
                        self.mm(pS[:, :].rearrange("p (j n) -> p j n", j=4), [(kap, qap)], reads=qrd,
                                writes=[("ps", sbk)])
                        self.A("act", lambda e: e.activation(out=ptk, in_=pS[:, :], func=AF.Exp, scale=0.125),
                               reads=[("ps", sbk)], writes=[("pt", pk)])
                        if msk is not None:
                            self.A("dve", lambda e: e.tensor_tensor(
                                out=ptk.rearrange("p (j n) -> p j n", j=4),
                                in0=ptk.rearrange("p (j n) -> p j n", j=4), in1=bc_mid(msk, 4), op=ALU.mult),
                                reads=[("pt", pk)], writes=[("pt", pk)])

                    def cc(po=po, lv=lv, ptk=ptk, pk=pk, ci=ci, nch=len(cl), vkey=vkey, ob=ob):
                        self.A("pe", lambda e: e.matmul(po[:, :], lhsT=lv, rhs=ptk, start=(ci == 0),
                                                        stop=(ci == nch - 1)),
                               reads=[vkey, ("pt", pk)], writes=[("ps", ob)])
                    it = dict(ab=ab, c=cc)
                    if ci == len(cl) - 1:
                        def sink_fn(oa_, oak, g=g):
                            sk = bc_last(self.esink[64:128, 8 * j + 4 * g:8 * j + 4 * g + 4], 128)
                            self.A("dve", lambda e: e.tensor_tensor(
                                out=oa_[64:128, :].rearrange("p (j n) -> p j n", j=4),
                                in0=oa_[64:128, :].rearrange("p (j n) -> p j n", j=4), in1=sk, op=ALU.add),
                                reads=[oak, "c_sink"], writes=[oak])
                        dst = mixedT[rows, 0:4, 128 * qb_:128 * qb_ + 128]
                        it["f1"] = (lambda po=po, ob=ob, k=k, sink_fn=sink_fn: self.fin1(
                            po[:, :], ob, oa[k], ("oa", k), sink_fn=sink_fn))
                        it["f2"] = (lambda rows=rows, dst=dst, k=k, sb=sb: self.fin2(
                            rows, dst, [("mixedT",)], oa[k], ("oa", k), sb))
                    items.append(it)
        self.run_pipe(items, 4, 6, 4)
        self.P.barrier()
        ar.off = mark_mla
        wqb = ar.alloc(3 * 768, BF16).rearrange("p (c n) -> p c n", c=3)
        wkvb = ar.alloc(2 * 1024, BF16).rearrange("p (c n) -> p c n", c=2)
        self.dma(wqb, self.w_qb[j].rearrange("(c p) n -> p c n", p=128), writes=[("wqb",)], eng="pool")
        self.dma(wkvb, self.w_kvb[j].rearrange("(c p) n -> p c n", p=128), writes=[("wkvb",)], eng="pool")
        c32 = [ar.alloc(512, F32) for _ in range(2)]
        s32 = [ar.alloc(512, F32) for _ in range(2)]
        qTh = [ar.alloc(S, BF16) for _ in range(2)]
        vb = [ar.alloc(24 * 128, BF16).rearrange("p (c n) -> p c n", c=24) for _ in range(0)]
        assert ar.off <= mark, (ar.off, mark)
        ar.off = mark2
        self.alloc_proj_tmps(rope64=False)
        vb = [ar.alloc(32 * 128, BF16).rearrange("p (c n) -> p c n", c=32) for _ in range(2)]
        pt = [ar.alloc(512, BF16) for _ in range(8)]
        oa = [ar.alloc(512, F32) for _ in range(2)]
        for tile in range(NT):
            pb = 6 + tile % 2
            pv_ = self.ps[pb]
            ts = slice(tile * 128, (tile + 1) * 128)

            def vsl(c):
                a = wkvb[:, c, 64:128]
                return bass.AP(a.tensor, a.offset, [list(a.ap[0]), [128, 8], [1, 64]])
            self.mm(pv_[:, :].rearrange("p (h d) -> p h d", h=8), [(kvaT[:, c, ts], vsl(c)) for c in range(2)],
                    reads=[("wkvb",)], writes=[("ps", pb)])
            self.v_store(pv_[:, :].rearrange("p (h d) -> p h d", h=8), pb, 8, tile)
        r32 = slice(64, 96)
        sc = float(96 ** -0.5)
        self.npj = 0

        def head_pre(h):
            hb = h % 2
            self.load_v(vb[hb], h, 1, ("vb", hb))
            for c in range(8):
                cs = slice(c * 512, (c + 1) * 512)
                cb = (8 * h + c) % 2
                self.dma(c32[cb][r32, :], self.rope32[0][r32, cs], writes=[("c32", cb)])
                self.dma(s32[cb][r32, :], self.rope32[1][r32, cs], writes=[("s32", cb)])
                pb = 1 + self.npj % 3
                rb = 6 + self.npj % 2
                self.npj += 1
                pq = self.ps[pb]
                self.mm(pq[0:96, :], [(wqb[:, cq, 96 * h:96 * h + 96], qaT[:, cq, cs]) for cq in range(3)],
                        reads=[("wqb",)], writes=[("ps", pb)])
                self.A("dve", lambda e, pq=pq, hb=hb, cs=cs: e.tensor_copy(out=qTh[hb][0:64, cs], in_=pq[0:64, :]),
                       reads=[("ps", pb)], writes=[("qTh", hb)])
                prev = self.rope_store(pq, r32, qTh[hb][r32, cs], c32[cb], s32[cb], self.R32, self.qb, self.t1,
                                       self.t2, [("c32", cb), ("s32", cb)], [("qTh", hb)], pb, rb)
                if prev is not None:
                    prev()
                pb = 1 + self.npj % 3
                self.npj += 1
                pk_ = self.ps[pb]
                self.mm(pk_[0:64, :], [(wkvb[:, cq, 128 * h:128 * h + 64], kvaT[:, cq, cs]) for cq in range(2)],
                        reads=[("wkvb",)], writes=[("ps", pb)])
                self.A("dve", lambda e, pk_=pk_, hb=hb, cs=cs: e.tensor_copy(out=kTh[hb][0:64, cs], in_=pk_[0:64, :]),
                       reads=[("ps", pb)], writes=[("kThn", hb)])
            self.rope_flush()

        items = []
        nit = 0
        grp = 0
        for h in range(8):
            hb = h % 2
            base = (h % 2) * 64
            rows = slice(base, base + 64)
            vbb = vb[hb]
            vkey = ("vb", hb)
            for qc in range(8):
                qs_ = slice(qc * 512, (qc + 1) * 512)
                ob = 4 + grp % 2
                k = grp % 2
                grp += 1
                po = self.ps[ob]
                for kc in range(NT):
                    sbk = nit % 4
                    pk = nit % 8
                    nit += 1
                    pS = self.ps[sbk]
                    ptk = pt[pk]
                    kap = kTh[hb][0:96, 128 * kc:128 * kc + 128]
                    qap = qTh[hb][0:96, qs_]
                    lv = vbb[:, kc, :]

                    def ab(pS=pS, kap=kap, qap=qap, sbk=sbk, pk=pk, ptk=ptk, hb=hb):
                        self.mm(pS[:, :], [(kap, qap)], reads=[("qTh", hb), ("kThn", hb)], writes=[("ps", sbk)])
                        self.A("act", lambda e: e.activation(out=ptk, in_=pS[:, :], func=AF.Exp, scale=sc),
                               reads=[("ps", sbk)], writes=[("pt", pk)])

                    def cc(po=po, lv=lv, ptk=ptk, pk=pk, kc=kc, vkey=vkey, ob=ob):
                        self.A("pe", lambda e: e.matmul(po[:, :], lhsT=lv, rhs=ptk, start=(kc == 0),
                                                        stop=(kc == NT - 1)),
                               reads=[vkey, ("pt", pk)], writes=[("ps", ob)])
                    it = dict(ab=ab, c=cc)
                    if kc == NT - 1:
                        dst = mixedT[rows, 4 + h // 2, qs_]
                        it["f1"] = (lambda po=po, ob=ob, k=k: self.fin1(po[:, :], ob, oa[k], ("oa", k), eng="dve"))
                        it["f2"] = (lambda rows=rows, dst=dst, k=k: self.fin2(rows, dst, [("mixedT",)], oa[k],
                                                                              ("oa", k), 6 + k))
                    if qc == 1 and kc == 0 and h + 1 < 8:
                        it["pre"] = (lambda h=h: head_pre(h + 1))
                    items.append(it)
        head_pre(0)
        self.run_pipe(items, 7, 12)
        self.P.barrier()
        ar.off = mark2
        wo = self.odd_w_out[j]
        w_rows = []
        for c in range(4):
            w_rows.append([(wo[64 * c:64 * c + 64, :], slice(0, 64)),
                           (wo[64 * (c + 4):64 * (c + 4) + 64, :], slice(64, 128))])
        for c in range(4, 8):
            w_rows.append([(wo[c * 128:(c + 1) * 128, :], slice(0, 128))])
        self.out_proj(mixedT, w_rows, src)


def _rope_tab(dim, nrows_fn):
    pos = np.arange(S, dtype=np.float32)
    inv = (np.float32(10000.0) ** (-np.arange(0, dim, 2, dtype=np.float32) / np.float32(dim))).astype(np.float32)
    ang = pos[None, :] * inv[:, None]
    return np.cos(ang).astype(np.float32), np.sin(ang).astype(np.float32)


def _constants():
    ident = np.eye(128, dtype=np.float32)
    swap = np.zeros((128, 128), np.float32)
    R64 = np.zeros((128, 128), np.float32)
    R32 = np.zeros((128, 128), np.float32)
    for m in range(128):
        swap[(m + 64) % 128, m] = 1.0
        jj = m % 64
        blk = m - jj
        if jj < 32:
            R64[blk + jj + 32, m] = -1.0
        else:
            R64[blk + jj - 32, m] = 1.0
    for m in range(64, 96):
        jj = m - 64
        if jj < 16:
            R32[m + 16, m] = -1.0
        else:
            R32[m - 16, m] = 1.0
    kk = np.arange(128)[:, None]
    qq = np.arange(128)[None, :]
    Mge = (kk >= qq).astype(np.float32)
    Mle = (kk <= qq).astype(np.float32)
    cst = np.stack([ident, swap, R64, R32, Mge, Mle]).astype(np.float32)
    c64, s64 = _rope_tab(64, None)
    rope64 = np.zeros((2, 128, S), np.float32)
    for p in range(128):
        rope64[0, p] = c64[p % 32]
        rope64[1, p] = s64[p % 32]
    c32, s32 = _rope_tab(32, None)
    rope32 = np.zeros((2, 128, S), np.float32)
    for p in range(64, 96):
        rope32[0, p] = c32[(p - 64) % 16]
        rope32[1, p] = s32[(p - 64) % 16]
    kr = np.arange(128) // 64
    kc = np.arange(128) % 64
    u = np.arange(896) // 64 - 6
    qc = np.arange(896) % 64
    dr = kr[:, None] - u[None, :]
    dc = kc[:, None] - qc[None, :]
    ws = np.clip(qc - 8, 0, 48)
    colv = (kc[:, None] >= ws[None, :]) & (kc[:, None] < ws[None, :] + 16)
    rowv_int = (dr >= -4) & (dr <= 3)
    rowv_full = (dr >= -7) & (dr <= 7)
    namask = np.stack([(colv & rowv_int), (colv & rowv_full)]).astype(np.float32)
    dri = np.clip(dr + 7, 0, 14)
    dci = np.clip(dc + 15, 0, 30)
    return cst, rope64, rope32, namask, dri, dci


_CACHE = {}


def _get_nc(nsteps=12, final=True):
    key = (nsteps, final)
    if key not in _CACHE:
        _CACHE[key] = Builder(nsteps, final).build()
    return _CACHE[key]


def _prep(inputs):
    f = lambda a: np.ascontiguousarray(np.asarray(a, dtype=np.float32))
    cst, rope64, rope32, namask, dri, dci = _constants()
    gains = np.zeros((128, 96), np.float32)
    for blk, name in enumerate(("ffn1_norm", "mix_norm", "ffn2_norm")):
        g = f(inputs[name])
        for i in range(4):
            gains[:, blk * 32 + 8 * i:blk * 32 + 8 * i + 8] = g[i].reshape(8, 128).T
    mla_g = np.zeros((128, 10), np.float32)
    qn, kn = f(inputs["mla_q_norm"]), f(inputs["mla_kv_norm"])
    for j in range(2):
        mla_g[:, 5 * j:5 * j + 3] = qn[j].reshape(3, 128).T
        mla_g[:, 5 * j + 3:5 * j + 5] = kn[j].reshape(2, 128).T
    rpb = f(inputs["na_rel_bias"])
    g = rpb[:, :, dri, dci]
    narpb = np.ascontiguousarray(np.stack([g, g], axis=2))
    shared = dict(
        gains=gains, fnorm=f(inputs["final_norm"]).reshape(1, D),
        ffn1_w1=f(inputs["ffn1_w1"]), ffn1_w3=f(inputs["ffn1_w3"]), ffn1_w2=f(inputs["ffn1_w2"]),
        ffn2_w1=f(inputs["ffn2_w1"]), ffn2_w3=f(inputs["ffn2_w3"]), ffn2_w2=f(inputs["ffn2_w2"]),
        even_w_in=f(inputs["even_w_in"]), even_w_out=f(inputs["even_w_out"]),
        odd_w_in=f(inputs["odd_w_in"]), odd_w_out=f(inputs["odd_w_out"]),
        mla_w_qb=f(inputs["mla_w_qb"]), mla_w_kvb=f(inputs["mla_w_kvb"]), mla_g=mla_g,
        sink=f(inputs["swa_sink"]).reshape(1, 16), narpb=narpb, namask=namask, cst=cst,
        rope64=rope64, rope32=rope32)
    return shared


def kernel(**inputs):
    nsteps = inputs.pop("_nsteps", 12)
    final = inputs.pop("_final", True)
    cores = inputs.pop("_cores", list(range(8)))
    nc = _get_nc(nsteps, final)
    shared = _prep(inputs)
    x = np.asarray(inputs["x"], dtype=np.float32)
    in_maps = []
    for b in cores:
        m = dict(shared)
        m["x"] = np.ascontiguousarray(x[b])
        in_maps.append(m)
    res = run_bass_kernel_spmd(nc, in_maps, core_ids=list(range(len(cores))))
    return np.stack([np.asarray(r["out"], dtype=np.float32) for r in res.results], axis=0)
```
